# Optimizing a Trainium2 kernel written in Bass

```python
import math
import jax, jax.numpy as jnp
from jax import lax
import numpy as np

D_MODEL = 1024
BATCH = 8
SEQ = 2048
DEPTH = 1
DEC_BATCH = 8
DEC_SEQ = 64
PAST_LEN = 1024

CHUNK = 64
WINDOW = 128
WIN_CHUNKS = WINDOW // CHUNK
N_HEADS = 8
N_KV_HEADS = 2
HEAD_DIM = 64
Q_PER_KV = N_HEADS // N_KV_HEADS
ATTN_WIDTH = N_HEADS * HEAD_DIM
KV_WIDTH = N_KV_HEADS * HEAD_DIM
ROT_DIM = HEAD_DIM // 4
ROPE_THETA = 500000.0
SSM_WIDTH = D_MODEL // 2
SSM_GROUP = 16
SSM_GROUPS = SSM_WIDTH // SSM_GROUP
SSM_STATE = 64
PLE_DIM = 256
N_BRANCH = 2
IN_WIDTH = 2 * ATTN_WIDTH + 2 * KV_WIDTH + 2 * SSM_WIDTH + N_BRANCH * D_MODEL
EPS = 1e-6

kernel_name = "hybrid_swa_sink_s5_streaming_step"


def rmsnorm(x, gain):
    xf = x.astype(jnp.float32)
    y = xf * lax.rsqrt(jnp.mean(xf * xf, axis=-1, keepdims=True) + EPS) * gain.astype(jnp.float32)
    return y.astype(x.dtype)


def rope_partial(x, pos):
    half = ROT_DIM // 2
    inv = jnp.power(ROPE_THETA, -jnp.arange(half, dtype=jnp.float32) * 2.0 / ROT_DIM)
    ang = pos.astype(jnp.float32)[:, None] * inv[None, :]
    cos = jnp.cos(ang)[None, :, None, :]
    sin = jnp.sin(ang)[None, :, None, :]
    xf = x.astype(jnp.float32)
    x1, x2, rest = xf[..., :half], xf[..., half:ROT_DIM], xf[..., ROT_DIM:]
    out = jnp.concatenate([x1 * cos - x2 * sin, x2 * cos + x1 * sin, rest], axis=-1)
    return out.astype(x.dtype)


def sink_attention(q, k, v, valid, sinks):
    f32 = jnp.float32
    s = jnp.einsum('bnqhgd,bnkhd->bnhgqk', q.astype(f32), k.astype(f32)) * (HEAD_DIM ** -0.5)
    if valid is not None:
        s = jnp.where(valid[None, :, None, None, None, :], s, -jnp.inf)
    sink = sinks.astype(f32).reshape(N_KV_HEADS, Q_PER_KV)[None, None, :, :, None, None]
    m = jnp.maximum(jnp.max(s, axis=-1, keepdims=True), sink)
    e = jnp.exp(s - m)
    w = e / (jnp.sum(e, axis=-1, keepdims=True) + jnp.exp(sink - m))
    o = jnp.einsum('bnhgqk,bnkhd->bnqhgd', w, v.astype(f32))
    return o.astype(q.dtype)


def prompt_attention(q, k, v, sinks):
    b, t = q.shape[0], q.shape[1]
    nc = t // CHUNK
    pad = WIN_CHUNKS * CHUNK
    qc = q.reshape(b, nc, CHUNK, N_KV_HEADS, Q_PER_KV, HEAD_DIM)

    def band(x):
        xp = jnp.pad(x, ((0, 0), (pad, 0), (0, 0), (0, 0)))
        xc = xp.reshape(b, nc + WIN_CHUNKS, CHUNK, N_KV_HEADS, HEAD_DIM)
        return jnp.concatenate([xc[:, j:j + nc] for j in range(WIN_CHUNKS + 1)], axis=2)

    key_pos = (jnp.arange(nc)[:, None] - WIN_CHUNKS) * CHUNK + jnp.arange((WIN_CHUNKS + 1) * CHUNK)[None, :]
    o = sink_attention(qc, band(k), band(v), key_pos >= 0, sinks)
    return o.reshape(b, t, ATTN_WIDTH), k[:, -WINDOW:], v[:, -WINDOW:]


def sample_attention(q, k, v, cache_k, cache_v, sinks):
    b, s = q.shape[0], q.shape[1]
    kk = jnp.concatenate([cache_k.astype(k.dtype), k], axis=1)
    vv = jnp.concatenate([cache_v.astype(v.dtype), v], axis=1)
    qs = q.reshape(b, 1, s, N_KV_HEADS, Q_PER_KV, HEAD_DIM)
    o = sink_attention(qs, kk[:, None], vv[:, None], None, sinks)
    return o.reshape(b, s, ATTN_WIDTH), kk[:, -WINDOW:], vv[:, -WINDOW:]


def ssm_branch(u, h0_re, h0_im, a_re, a_im, log_dt, b_re, b_im, c_re, c_im, d_skip, w_glu):
    f32 = jnp.float32
    bsz, t, _ = u.shape
    lam = lax.complex(a_re.astype(f32), a_im.astype(f32))
    dt = jnp.exp(log_dt.astype(f32))[:, None]
    a_bar = jnp.exp(lam * dt)
    bmat = lax.complex(b_re.astype(f32), b_im.astype(f32))
    b_bar = ((a_bar - 1.0) / lam)[..., None] * bmat
    cmat = lax.complex(c_re.astype(f32), c_im.astype(f32))
    uf = u.astype(f32)
    ug = uf.reshape(bsz, t, SSM_GROUPS, SSM_GROUP).astype(jnp.complex64)
    bu = jnp.einsum('gpc,btgc->btgp', b_bar, ug)
    if h0_re is not None:
        h0 = lax.complex(h0_re.astype(f32), h0_im.astype(f32))
        bu = bu.at[:, 0].add(a_bar[None] * h0)
    a_seq = jnp.broadcast_to(a_bar, bu.shape)

    def combine(left, right):
        a1, b1 = left
        a2, b2 = right
        return a1 * a2, a2 * b1 + b2

    _, h = lax.associative_scan(combine, (a_seq, bu), axis=1)
    y = jnp.einsum('gcp,btgp->btgc', cmat, h).real.reshape(bsz, t, SSM_WIDTH)
    y = y + d_skip.astype(f32) * uf
    z = jax.nn.gelu(y)
    z = z * jax.nn.sigmoid(z @ w_glu.astype(f32))
    h_last = h[:, -1]
    return z.astype(u.dtype), h_last.real, h_last.imag


def trunk_layer(h, p, pos, attend, h0_re, h0_im, norm_gain, w_in, w_o_attn,
                ssm_a_re, ssm_a_im, ssm_log_dt, ssm_b_re, ssm_b_im, ssm_c_re, ssm_c_im,
                ssm_d, ssm_w_glu, w_o_ssm, w_out, w_ple_gate, w_ple_proj):
    b, t, _ = h.shape
    xn = rmsnorm(h, norm_gain)
    z = xn @ w_in
    o0 = ATTN_WIDTH
    o1 = o0 + KV_WIDTH
    o2 = o1 + KV_WIDTH
    o3 = o2 + ATTN_WIDTH
    o4 = o3 + SSM_WIDTH
    o5 = o4 + SSM_WIDTH
    o6 = o5 + D_MODEL
    q = rope_partial(z[..., :o0].reshape(b, t, N_HEADS, HEAD_DIM), pos)
    k = rope_partial(z[..., o0:o1].reshape(b, t, N_KV_HEADS, HEAD_DIM), pos)
    v = z[..., o1:o2].reshape(b, t, N_KV_HEADS, HEAD_DIM)
    z_attn = z[..., o2:o3]
    u = z[..., o3:o4]
    z_ssm = z[..., o4:o5]
    g_attn = z[..., o5:o6]
    g_ssm = z[..., o6:]

    attn, k_state, v_state = attend(q, k, v)
    ssm, s_re, s_im = ssm_branch(u, h0_re, h0_im, ssm_a_re, ssm_a_im, ssm_log_dt, ssm_b_re,
                                 ssm_b_im, ssm_c_re, ssm_c_im, ssm_d, ssm_w_glu)
    ya = (attn * jax.nn.silu(z_attn)) @ w_o_attn
    ys = (ssm * jax.nn.silu(z_ssm)) @ w_o_ssm
    merged = jax.nn.sigmoid(g_attn) * ya + jax.nn.sigmoid(g_ssm) * ys
    h = h + merged @ w_out
    h = h + jax.nn.sigmoid(h @ w_ple_gate) * (p @ w_ple_proj)
    return h, k_state, v_state, s_re, s_im


def setup_inputs(seed: int = 0) -> dict:
    key = jax.random.key(seed)
    ks = jax.random.split(key, 32)
    nrm = jax.random.normal
    f32 = jnp.float32
    d = {}
    d["x_prompt"] = nrm(ks[0], (BATCH, SEQ, D_MODEL), f32)
    d["x_sample"] = nrm(ks[1], (DEC_BATCH, DEC_SEQ, D_MODEL), f32)
    d["p_prompt"] = nrm(ks[2], (DEPTH, BATCH, SEQ, PLE_DIM), f32)
    d["p_sample"] = nrm(ks[3], (DEPTH, DEC_BATCH, DEC_SEQ, PLE_DIM), f32)
    d["cache_attn_k"] = nrm(ks[4], (DEPTH, DEC_BATCH, WINDOW, N_KV_HEADS, HEAD_DIM), f32)
    d["cache_attn_v"] = nrm(ks[5], (DEPTH, DEC_BATCH, WINDOW, N_KV_HEADS, HEAD_DIM), f32)
    d["state_ssm_re"] = 0.1 * nrm(ks[6], (DEPTH, DEC_BATCH, SSM_GROUPS, SSM_STATE), f32)
    d["state_ssm_im"] = 0.1 * nrm(ks[7], (DEPTH, DEC_BATCH, SSM_GROUPS, SSM_STATE), f32)
    d["norm_gain"] = 1.0 + 0.02 * nrm(ks[8], (DEPTH, D_MODEL), f32)
    d["w_in"] = nrm(ks[9], (DEPTH, D_MODEL, IN_WIDTH), f32) * D_MODEL ** -0.5
    d["attn_sinks"] = 0.5 * nrm(ks[10], (DEPTH, N_HEADS), f32)
    d["w_o_attn"] = nrm(ks[11], (DEPTH, ATTN_WIDTH, D_MODEL), f32) * ATTN_WIDTH ** -0.5
    d["ssm_a_re"] = -0.5 + 0.01 * nrm(ks[12], (DEPTH, SSM_GROUPS, SSM_STATE), f32)
    d["ssm_a_im"] = (math.pi * jnp.arange(SSM_STATE, dtype=f32))[None, None, :] + 0.01 * nrm(ks[13], (DEPTH, SSM_GROUPS, SSM_STATE), f32)
    d["ssm_log_dt"] = jax.random.uniform(ks[14], (DEPTH, SSM_GROUPS), f32, math.log(1e-3), math.log(1e-1))
    d["ssm_b_re"] = nrm(ks[15], (DEPTH, SSM_GROUPS, SSM_STATE, SSM_GROUP), f32) * (2 * SSM_GROUP) ** -0.5
    d["ssm_b_im"] = nrm(ks[16], (DEPTH, SSM_GROUPS, SSM_STATE, SSM_GROUP), f32) * (2 * SSM_GROUP) ** -0.5
    d["ssm_c_re"] = nrm(ks[17], (DEPTH, SSM_GROUPS, SSM_GROUP, SSM_STATE), f32) * SSM_STATE ** -0.5
    d["ssm_c_im"] = nrm(ks[18], (DEPTH, SSM_GROUPS, SSM_GROUP, SSM_STATE), f32) * SSM_STATE ** -0.5
    d["ssm_d"] = nrm(ks[19], (DEPTH, SSM_WIDTH), f32)
    d["ssm_w_glu"] = nrm(ks[20], (DEPTH, SSM_WIDTH, SSM_WIDTH), f32) * SSM_WIDTH ** -0.5
    d["w_o_ssm"] = nrm(ks[21], (DEPTH, SSM_WIDTH, D_MODEL), f32) * SSM_WIDTH ** -0.5
    d["w_out"] = nrm(ks[22], (DEPTH, D_MODEL, D_MODEL), f32) * D_MODEL ** -0.5
    d["w_ple_gate"] = nrm(ks[23], (DEPTH, D_MODEL, D_MODEL), f32) * D_MODEL ** -0.5
    d["w_ple_proj"] = nrm(ks[24], (DEPTH, PLE_DIM, D_MODEL), f32) * PLE_DIM ** -0.5
    d["final_norm_gain"] = 1.0 + 0.02 * nrm(ks[25], (D_MODEL,), f32)
    return d


def reference(x_prompt, x_sample, p_prompt, p_sample, cache_attn_k, cache_attn_v,
              state_ssm_re, state_ssm_im, norm_gain, w_in, attn_sinks, w_o_attn,
              ssm_a_re, ssm_a_im, ssm_log_dt, ssm_b_re, ssm_b_im, ssm_c_re, ssm_c_im,
              ssm_d, ssm_w_glu, w_o_ssm, w_out, w_ple_gate, w_ple_proj, final_norm_gain):
    pos_p = jnp.arange(x_prompt.shape[1])
    pos_s = PAST_LEN + jnp.arange(x_sample.shape[1])
    h_p, h_s = x_prompt, x_sample
    kp_l, vp_l, rp_l, ip_l, ks_l, vs_l, rs_l, is_l = [], [], [], [], [], [], [], []
    for i in range(DEPTH):
        lw = (norm_gain[i], w_in[i], w_o_attn[i], ssm_a_re[i], ssm_a_im[i], ssm_log_dt[i],
              ssm_b_re[i], ssm_b_im[i], ssm_c_re[i], ssm_c_im[i], ssm_d[i], ssm_w_glu[i],
              w_o_ssm[i], w_out[i], w_ple_gate[i], w_ple_proj[i])
        sinks = attn_sinks[i]
        ck, cv = cache_attn_k[i], cache_attn_v[i]
        h_p, kp, vp, rp, ip = trunk_layer(
            h_p, p_prompt[i], pos_p,
            lambda q, k, v: prompt_attention(q, k, v, sinks),
            None, None, *lw)
        h_s, ks_, vs_, rs_, is_ = trunk_layer(
            h_s, p_sample[i], pos_s,
            lambda q, k, v: sample_attention(q, k, v, ck, cv, sinks),
            state_ssm_re[i], state_ssm_im[i], *lw)
        kp_l.append(kp); vp_l.append(vp); rp_l.append(rp); ip_l.append(ip)
        ks_l.append(ks_); vs_l.append(vs_); rs_l.append(rs_); is_l.append(is_)
    y_prompt = rmsnorm(h_p, final_norm_gain)
    y_sample = rmsnorm(h_s, final_norm_gain)
    new_k_prompt = jnp.stack(kp_l)
    new_v_prompt = jnp.stack(vp_l)
    new_ssm_re_prompt = jnp.stack(rp_l)
    new_ssm_im_prompt = jnp.stack(ip_l)
    new_k_sample = jnp.stack(ks_l)
    new_v_sample = jnp.stack(vs_l)
    new_ssm_re_sample = jnp.stack(rs_l)
    new_ssm_im_sample = jnp.stack(is_l)
    return (y_prompt, y_sample, new_k_prompt, new_v_prompt, new_ssm_re_prompt, new_ssm_im_prompt,
            new_k_sample, new_v_sample, new_ssm_re_sample, new_ssm_im_sample)
```

```python
import math
from contextlib import ExitStack

import numpy as np
import ml_dtypes
import concourse.bass as bass
import concourse.mybir as mybir
from concourse.bass_utils import run_bass_kernel_spmd

F32 = mybir.dt.float32
BF16 = mybir.dt.bfloat16
AF = mybir.ActivationFunctionType
ALU = mybir.AluOpType

NT = 17
NTOK = NT * 128
EPS = 1e-6
PI = math.pi
DEBUG = {}
ATTACH_WAITS = True


class T:
    __slots__ = ("ap", "name", "lw", "rs")

    def __init__(self, ap, name=""):
        self.ap = ap
        self.name = name
        self.lw = None
        self.rs = []

    def __getitem__(self, k):
        return self.ap[k]


class Op:
    __slots__ = ("eng", "fn", "idx", "deps", "dsem", "dval", "isdma", "name", "bar")


class Prog:
    ENGS = ("pe", "act", "dve", "pool", "sp")

    def __init__(self, nc, n_dma_sems=48):
        self.nc = nc
        self.ops = {e: [] for e in self.ENGS}
        self.n_dma_sems = n_dma_sems
        self.dma_rr = 0
        self.dma_rr_sw = 0
        self.dma_cnt = [0] * n_dma_sems
        self.out_dmas = []
        self.ccnt = {e: 0 for e in self.ENGS}

    def _add(self, eng, fn, reads, writes, isdma=False, name=""):
        op = Op()
        op.eng = eng
        op.fn = fn
        op.isdma = isdma
        op.name = name
        op.bar = None
        deps = []
        for t in reads:
            if t.lw is not None:
                deps.append(t.lw)
        for t in writes:
            if t.lw is not None:
                deps.append(t.lw)
            deps.extend(t.rs)
        for t in writes:
            t.lw = op
            t.rs = []
        for t in reads:
            if t.lw is not op:
                t.rs.append(op)
        op.deps = deps
        self.ops[eng].append(op)
        if not isdma:
            self.ccnt[eng] += 1
        op.idx = self.ccnt[eng]
        if isdma:
            nsw = self.n_dma_sems // 3
            if eng == "pool":
                s = self.dma_rr_sw
                self.dma_rr_sw = (self.dma_rr_sw + 1) % nsw
            else:
                s = nsw + self.dma_rr
                self.dma_rr = (self.dma_rr + 1) % (self.n_dma_sems - nsw)
            self.dma_cnt[s] += 16
            op.dsem = s
            op.dval = self.dma_cnt[s]
        return op

    def pe(self, fn, reads, writes, name=""):
        return self._add("pe", fn, reads, writes, name=name)

    def act(self, fn, reads, writes, name=""):
        return self._add("act", fn, reads, writes, name=name)

    def dve(self, fn, reads, writes, name=""):
        return self._add("dve", fn, reads, writes, name=name)

    def pool(self, fn, reads, writes, name=""):
        return self._add("pool", fn, reads, writes, name=name)

    def dma(self, out, in_, reads, writes, q="sp", is_out=False, slow=False, name=""):
        def fn(e):
            if slow:
                return e.dma_start(out=out, in_=in_, allow_slow_non_contiguous=True)
            return e.dma_start(out=out, in_=in_)
        op = self._add(q, fn, reads, writes, isdma=True, name=name)
        if is_out:
            self.out_dmas.append(op)
        return op

    def barrier(self):
        snap_c = dict(self.ccnt)
        snap_d = list(self.dma_cnt)
        for e in self.ENGS:
            op = Op()
            op.eng = e
            op.fn = None
            op.isdma = False
            op.bar = (snap_c, snap_d)
            op.deps = []
            op.idx = self.ccnt[e]
            op.name = "barrier"
            self.ops[e].append(op)

    def emit(self):
        nc = self.nc
        with ExitStack() as st:
            esem = {e: st.enter_context(nc.semaphore("s_" + e)) for e in self.ENGS}
            dsem = [st.enter_context(nc.semaphore("d%d" % i)) for i in range(self.n_dma_sems)]
            block = st.enter_context(nc.Block())
            prog = self

            def run(ename, e):
                waited = {}

                def w(sem, val):
                    if val <= 0:
                        return
                    k = sem.name
                    if waited.get(k, 0) >= val:
                        return
                    waited[k] = val
                    e.wait_ge(sem, val)

                for op in prog.ops[ename]:
                    if op.bar is not None:
                        snap_c, snap_d = op.bar
                        for en in prog.ENGS:
                            if en != ename and en != "sp":
                                w(esem[en], snap_c[en])
                        for i in range(prog.n_dma_sems):
                            w(dsem[i], snap_d[i])
                        continue
                    pend = []

                    def need(sem, val):
                        if val <= 0:
                            return
                        k = sem.name
                        if waited.get(k, 0) >= val:
                            return
                        waited[k] = val
                        pend.append((sem, val))
                    for d in op.deps:
                        if d.isdma:
                            need(dsem[d.dsem], d.dval)
                        else:
                            if d.eng == "pe" and ename == "pe":
                                continue
                            need(esem[d.eng], d.idx)
                    best = {}
                    for sem, val in pend:
                        if sem.name not in best or best[sem.name][1] < val:
                            best[sem.name] = (sem, val)
                    pend = list(best.values())
                    attach = None
                    if ATTACH_WAITS and (not op.isdma) and ename in ("act", "dve", "pool", "pe") and pend:
                        attach = pend.pop()
                    for sem, val in pend:
                        e.wait_ge(sem, val)
                    if op.isdma:
                        w(dsem[op.dsem], op.dval - 16)
                        ins = op.fn(e)
                        ins.then_inc(dsem[op.dsem], 16)
                    elif ename == "pe" and attach is not None:
                        ins = op.fn(_FirstWait(e, attach[0], attach[1]))
                        ins.then_inc(esem[ename], 1)
                    else:
                        ins = op.fn(e)
                        if attach is not None:
                            ins = ins._wait_ge(attach[0], attach[1])
                        ins.then_inc(esem[ename], 1)
                if ename == "sp":
                    for op in prog.out_dmas:
                        w(dsem[op.dsem], op.dval)

            @block.sync
            def _(e):
                run("sp", e)

            @block.tensor
            def _(e):
                run("pe", e)

            @block.scalar
            def _(e):
                run("act", e)

            @block.vector
            def _(e):
                run("dve", e)

            @block.gpsimd
            def _(e):
                run("pool", e)


class _FirstWait:
    def __init__(self, e, sem, val):
        self._e, self._sem, self._val, self._done = e, sem, val, False

    def _wrap(self, ins):
        if not self._done:
            self._done = True
            ins = ins._wait_ge(self._sem, self._val)
        return ins

    def matmul(self, *a, **k):
        return self._wrap(self._e.matmul(*a, **k))

    def transpose(self, *a, **k):
        return self._wrap(self._e.transpose(*a, **k))


def bc_last(ap, n):
    dims = [list(d) for d in ap.ap]
    assert dims[-1][1] == 1
    dims[-1] = [0, n]
    return bass.AP(ap.tensor, ap.offset, dims)


def dram_bc_rows(ap1d, nparts):
    dims = [list(d) for d in ap1d.ap]
    return bass.AP(ap1d.tensor, ap1d.offset, [[0, nparts]] + dims)


class _Stop(Exception):
    pass


def build(stop_after=None, dbg=(), ntiles=NT, kstop=0):
    try:
        return _build(stop_after, dbg, ntiles, kstop)
    except _Stop as ex:
        P, nc = ex.args
        P.emit()
        return nc


def _build(stop_after, dbg, ntiles, kstop):
    nc = bass.Bass("TRN2", target_bir_lowering=False)
    P = Prog(nc)

    def din(name, shape, dt=F32):
        return nc.dram_tensor(name, list(shape), dt, kind="ExternalInput").ap()

    def dout(name, shape, dt=F32):
        return nc.dram_tensor(name, list(shape), dt, kind="ExternalOutput").ap()

    xall = din("xall", [NTOK, 1024])
    pall = din("pall", [NTOK, 256])
    ck = din("ck", [128, 128])
    cv = din("cv", [128, 128])
    h0r = din("h0r", [128, 16])
    h0i = din("h0i", [128, 16])
    norm_gain = din("norm_gain", [1024])
    fgain = din("fgain", [1024])
    w_in = din("w_in", [1024, 4352])
    sinks = din("sinks", [8])
    w_oa = din("w_oa", [512, 1024])
    a_re = din("a_re", [128, 16])
    a_im = din("a_im", [128, 16])
    log_dt = din("log_dt", [128, 16])
    b_re = din("b_re", [128, 256])
    b_im = din("b_im", [128, 256])
    c_re = din("c_re", [128, 256])
    c_im = din("c_im", [128, 256])
    ssm_d = din("ssm_d", [128, 16])
    w_glu = din("w_glu", [512, 512])
    w_os = din("w_os", [512, 1024])
    w_out = din("w_out", [1024, 1024])
    w_pg = din("w_pg", [1024, 1024])
    w_pp = din("w_pp", [256, 1024])
    ident_b = din("ident_b", [128, 128], BF16)
    ident_f = din("ident_f", [128, 128])
    ropetok = din("ropetok", [NTOK, 16])
    maskT = din("maskT", [128, 128])
    kidx = din("kidx", [64])
    jtab = din("jtab", [8])

    y_o = dout("y", [NTOK, 1024])
    kp_o = dout("kp", [128, 128])
    vp_o = dout("vp", [128, 128])
    srp_o = dout("srp", [128, 16])
    sip_o = dout("sip", [128, 16])
    ks_o = dout("ks", [128, 128])
    vs_o = dout("vs", [128, 128])
    srs_o = dout("srs", [128, 16])
    sis_o = dout("sis", [128, 16])
    dbg_o = {}
    for name, shape in dbg:
        dbg_o[name] = dout(name, shape)

    def lg_ap(d1, off=0):
        return bass.AP(d1.tensor, off, [[1, 128], [128, 16]])

    with ExitStack() as ST:
        def sb(name, shape, dt=F32, st=None):
            return (st or ST).enter_context(nc.sbuf_tensor(name, list(shape), dt))

        banks = []
        for i in range(8):
            pt = ST.enter_context(nc.psum_tensor("bank%d" % i, [128, 512], F32))
            banks.append(T(pt, "bank%d" % i))
        bank_rr = [0]

        def nbank():
            b = banks[bank_rr[0]]
            bank_rr[0] = (bank_rr[0] + 1) % 8
            return b

        identb = sb("identb", [128, 128], BF16); Tidb = T(identb)
        identf = sb("identf", [128, 128]); Tidf = T(identf)
        gain_bc = sb("gain_bc", [128, 1024]); Tgain = T(gain_bc)

        def norm_front(tt, xt, Txt, xn, Txn, ss, Tss, junk, Tjunk, dma=True, part=3):
            if dma:
                P.dma(xt[:], xall[tt * 128:(tt + 1) * 128, :], [], [Txt])
            if part & 1:
                P.act(lambda e: e.activation(junk[:], xt[:], AF.Square, scale=1.0 / 32, accum_out=ss[:, 0:1]), [Txt], [Tjunk, Tss])
                P.act(lambda e: e.activation(ss[:, 3:4], ss[:, 0:1], AF.Ln, bias=EPS), [Tss], [Tss])
                P.act(lambda e: e.activation(ss[:, 2:3], ss[:, 3:4], AF.Exp, scale=-0.5), [Tss], [Tss])
            if not (part & 2):
                return
            P.dve(lambda e: e.scalar_tensor_tensor(xn[:], xt[:], ss[:, 2:3], gain_bc[:], ALU.mult, ALU.mult),
                  [Txt, Tss, Tgain], [Txn])

        def norm_back(tt, xn, Txn, xnT, TxnT, col0, xnTp=None, TxnTp=None):
            bk = nbank()
            pb = bk.ap[:].bitcast(BF16)

            def tr(e):
                r = None
                for k in range(8):
                    r = e.transpose(pb[:, k * 128:(k + 1) * 128], xn[:, k * 128:(k + 1) * 128], identb[:])
                return r
            P.pe(tr, [Txn, Tidb], [bk])
            P.act(lambda e: e.activation(xnT[:, :, col0:col0 + 128],
                                         pb[:, 0:1024].rearrange("p (k t) -> p k t", k=8), AF.Copy),
                  [bk], [TxnT])
            if xnTp is not None:
                P.act(lambda e: e.activation(xnTp[:].rearrange("p k (t c) -> p k t c", t=4),
                                             pb[:, 0:1024].rearrange("p (k c t) -> p k t c", k=8, t=4), AF.Copy),
                      [bk], [TxnTp])

        def load_norm_tile(tt, xt, Txt, xn, Txn, xnT, TxnT, col0, ss, Tss, junk, Tjunk, xnTp=None, TxnTp=None):
            norm_front(tt, xt, Txt, xn, Txn, ss, Tss, junk, Tjunk)
            norm_back(tt, xn, Txn, xnT, TxnT, col0, xnTp, TxnTp)

        attnT = sb("attnT", [128, 4, NTOK], BF16)
        TattnT = [[T(attnT, "attnT%d_%d" % (c, t)) for t in range(NT)] for c in range(4)]
        zT = sb("zT", [128, 4, NTOK], BF16)
        TzT = [T(zT, "zT%d" % t) for t in range(NT)]
        with ExitStack() as S1:
            def walloc(name, kch, ncols, st):
                t = sb(name, [128, kch, ncols], BF16, st)
                return t, T(t, name)
            wq, Twq = walloc("wq", 8, 512, S1)
            wkv, Twkv = walloc("wkv", 8, 256, S1)
            wu, Twu = walloc("wu", 8, 512, S1)


            pre = {}
            for _n, _sh, _dt in (("tb", [128, 40, 16], F32), ("Tbf", [128, 16, 128], BF16), ("PXr", [128, 16, 128], BF16),
                                 ("PXi", [128, 16, 128], BF16), ("QRb", [128, 16, 128], BF16), ("QIb", [128, 16, 128], BF16),
                                 ("COS", [128, 16, 32], F32), ("SIN", [128, 16, 32], F32), ("RHO", [128, 16, 32], F32),
                                 ("h0rs", [128, 16], F32),
                                 ("h0is", [128, 16], F32), ("pw", [128, 5, 8, 16], F32)):
                pre[_n] = sb(_n, _sh, _dt, S1)
            SSbox = [ExitStack()]

            def s1(name, shape, dt=F32):
                if name in pre:
                    return pre[name]
                return sb(name, shape, dt, SSbox[0] if SSbox[0] is not None else S1)

            P.dma(identf[:], ident_f[:], [], [Tidf])
            are = s1("are", [128, 16]); aim = s1("aim", [128, 16]); ldt = s1("ldt", [128, 16])
            Tare, Taim, Tldt = T(are), T(aim), T(ldt)
            P.dma(are[:], a_re[:], [], [Tare])
            P.dma(aim[:], a_im[:], [], [Taim], q="act")
            P.dma(ldt[:], log_dt[:], [], [Tldt])
            jt = s1("jt", [128, 8]); Tjt = T(jt)
            P.dma(jt[:], dram_bc_rows(jtab, 128), [], [Tjt], q="act")
            kix = s1("kix", [128, 64]); Tkix = T(kix)
            P.dma(kix[:], dram_bc_rows(kidx, 128), [], [Tkix], q="act")
            bre = s1("bre", [128, 16, 16]); bim = s1("bim", [128, 16, 16]); Tbre, Tbim = T(bre), T(bim)
            P.dma(bre[:].rearrange("p g c -> p (g c)"), b_re[:], [], [Tbre])
            P.dma(bim[:].rearrange("p g c -> p (g c)"), b_im[:], [], [Tbim], q="act")
            crT = s1("crT", [128, 16, 16]); ciT = s1("ciT", [128, 16, 16]); TcrT, TciT = T(crT), T(ciT)
            P.dma(crT[:].rearrange("p g c -> p (g c)"), c_re[:], [], [TcrT])
            P.dma(ciT[:].rearrange("p g c -> p (g c)"), c_im[:], [], [TciT], q="act")
            dcol = s1("dcol", [128, 16]); Tdcol = T(dcol)
            P.dma(dcol[:], ssm_d[:], [], [Tdcol])
            P.dma(identb[:], ident_b[:], [], [Tidb])
            P.dma(gain_bc[:], dram_bc_rows(norm_gain, 128), [], [Tgain])
            maskt = s1("maskt", [128, 128]); Tmaskt = T(maskt)
            P.dma(maskt[:], maskT[:], [], [Tmaskt])

            _param_tiles = [Tare, Taim, Tldt, Tbre, Tbim, TcrT, TciT, Tdcol]
            for (_t, _T, _src) in ((wq, Twq, w_in[:, 0:512]), (wkv, Twkv, w_in[:, 512:768]), (wu, Twu, w_in[:, 1280:1792])):
                P.dma(_t[:], _src.rearrange("(k p) n -> p k n", p=128), _param_tiles, [_T], q="pool")

            tb = s1("tb", [128, 40, 16]); Ttb = T(tb)
            _slot = [0]

            def slot():
                i = _slot[0]
                _slot[0] += 1
                assert i < 40
                return tb[:, i, :]

            def V(fn):
                P.dve(fn, [Ttb, Tare, Taim, Tldt], [Ttb])

            def A(fn):
                P.act(fn, [Ttb, Tare, Taim, Tldt], [Ttb])

            dt_ = slot(); A(lambda e: e.activation(dt_, ldt[:], AF.Exp))
            ard = slot(); V(lambda e: e.tensor_tensor(ard, are[:], dt_, ALU.mult))
            aid = slot(); V(lambda e: e.tensor_tensor(aid, aim[:], dt_, ALU.mult))

            def sincos(ang_fns, shape_like, sin_out, cos_out, tmp1, tmp2, V_, A_):
                for (dst, shift) in ((sin_out, 0.0), (cos_out, PI / 2)):
                    for f in ang_fns(shift):
                        V_(f)
                    V_(lambda e: e.tensor_scalar(tmp2, tmp1, 1.0 / (2 * PI), 12582912.0, ALU.mult, ALU.add))
                    V_(lambda e: e.tensor_scalar(tmp2, tmp2, -12582912.0, None, ALU.add))
                    V_(lambda e: e.scalar_tensor_tensor(tmp1, tmp2, -2 * PI, tmp1, ALU.mult, ALU.add))
                    V_(lambda e: e.tensor_scalar(tmp1, tmp1, PI, -PI, ALU.min, ALU.max))
                    A_(lambda e, dst=dst: e.activation(dst, tmp1, AF.Sin))

            JL = (-1, -2, -3, -4, 1, 2, 3, 4)
            pw = s1("pw", [128, 5, 8, 16]); Tpw = T(pw)
            jb = bc_last(jt[:].rearrange("p (j o) -> p j o", o=1), 16)
            ardb = bass.AP(tb, ard.offset, [list(ard.ap[0]), [0, 8], [1, 16]])
            aidb = bass.AP(tb, aid.offset, [list(aid.ap[0]), [0, 8], [1, 16]])
            t1, t2 = slot(), slot()

            def Vp(fn):
                P.dve(fn, [Ttb, Tjt, Tpw], [Tpw])

            def Ap(fn):
                P.act(fn, [Tpw], [Tpw])
            Vp(lambda e: e.tensor_tensor(pw[:, 2], jb, ardb, ALU.mult))
            Ap(lambda e: e.activation(pw[:, 2], pw[:, 2], AF.Exp))

            def angj(sh):
                return [lambda e: e.tensor_tensor(pw[:, 0], jb, aidb, ALU.mult),
                        lambda e, sh=sh: e.tensor_scalar(pw[:, 0], pw[:, 0], sh, None, ALU.add)]
            sincos(angj, None, pw[:, 3], pw[:, 4], pw[:, 0], pw[:, 1], Vp, Ap)
            Vp(lambda e: e.tensor_tensor(pw[:, 3], pw[:, 3], pw[:, 2], ALU.mult))
            Vp(lambda e: e.tensor_tensor(pw[:, 4], pw[:, 4], pw[:, 2], ALU.mult))
            pr, pim = {}, {}
            for ji, j in enumerate(JL):
                pr[j], pim[j] = pw[:, 4, ji, :], pw[:, 3, ji, :]
            V(lambda e: e.tensor_copy(t1, pw[:, 4, 7, :]))
            P.dve(lambda e: e.tensor_copy(t2, pw[:, 3, 7, :]), [Tpw, Ttb], [Ttb])
            den = slot(); wr = slot(); wi = slot(); am1 = slot()
            V(lambda e: e.tensor_tensor(den, are[:], are[:], ALU.mult))
            V(lambda e: e.tensor_tensor(t1, aim[:], aim[:], ALU.mult))
            V(lambda e: e.tensor_tensor(den, den, t1, ALU.add))
            V(lambda e: e.reciprocal(den, den))
            V(lambda e: e.tensor_scalar(am1, pr[1], -1.0, None, ALU.add))
            V(lambda e: e.tensor_tensor(wr, am1, are[:], ALU.mult))
            V(lambda e: e.tensor_tensor(t1, pim[1], aim[:], ALU.mult))
            V(lambda e: e.tensor_tensor(wr, wr, t1, ALU.add))
            V(lambda e: e.tensor_tensor(wr, wr, den, ALU.mult))
            V(lambda e: e.tensor_tensor(wi, pim[1], are[:], ALU.mult))
            V(lambda e: e.tensor_tensor(t1, am1, aim[:], ALU.mult))
            V(lambda e: e.tensor_tensor(wi, wi, t1, ALU.subtract))
            V(lambda e: e.tensor_tensor(wi, wi, den, ALU.mult))

            tmpA = s1("tmpA", [128, 16, 16]); tmpB = s1("tmpB", [128, 16, 16]); Ttmp = T(tmpA)

            def cmul(dst_r, dst_i, Tdst, sr, si, xr, xi, Tx):
                srb = bc_last(sr.rearrange("p (g o) -> p g o", o=1), 16)
                sib = bc_last(si.rearrange("p (g o) -> p g o", o=1), 16)
                rd = [Ttb, Ttmp] + Tx
                P.dve(lambda e: e.tensor_tensor(tmpA[:], xr, srb, ALU.mult), rd, [Ttmp])
                P.dve(lambda e: e.tensor_tensor(tmpB[:], xi, sib, ALU.mult), rd, [Ttmp])
                P.dve(lambda e: e.tensor_tensor(dst_r, tmpA[:], tmpB[:], ALU.subtract), [Ttmp], Tdst)
                P.dve(lambda e: e.tensor_tensor(tmpA[:], xi, srb, ALU.mult), rd, [Ttmp])
                P.dve(lambda e: e.tensor_tensor(tmpB[:], xr, sib, ALU.mult), rd, [Ttmp])
                P.dve(lambda e: e.tensor_tensor(dst_i, tmpA[:], tmpB[:], ALU.add), [Ttmp], Tdst)

            bbr = s1("bbr", [128, 16, 16]); bbi = s1("bbi", [128, 16, 16]); Tbb = T(bbr)
            cmul(bbr[:], bbi[:], [Tbb], wr, wi, bre[:], bim[:], [Tbre, Tbim])

            PPr = s1("PPr", [128, 16, 128]); PPi = s1("PPi", [128, 16, 128])
            PTr = s1("PTr", [128, 16, 128]); PTi = s1("PTi", [128, 16, 128])
            QRf = s1("QRf", [128, 16, 128]); QIf = s1("QIf", [128, 16, 128])
            TPP, TPT, TQ = T(PPr), T(PTr), T(QRf)
            for m, Tm in ((PPr, TPP), (PPi, TPP), (PTr, TPT), (PTi, TPT), (QRf, TQ), (QIf, TQ)):
                P.pool(lambda e, m=m: e.memset(m[:], 0.0), [], [Tm])
            c4r = s1("c4r", [128, 16, 4, 16]); c4i = s1("c4i", [128, 16, 4, 16]); Tc4 = T(c4r)
            p4r = s1("p4r", [128, 16, 4, 16]); p4i = s1("p4i", [128, 16, 4, 16]); Tp4 = T(p4r)
            u4a = s1("u4a", [128, 16, 4, 16]); u4b = s1("u4b", [128, 16, 4, 16]); Tu4 = T(u4a)
            pst = list(pw[:].ap[0])

            def pw4(comp, j0):
                off = pw[:].offset + comp * 128 + j0 * 16
                return bass.AP(pw, off, [pst, [1, 16], [16, 4], [0, 16]])

            def bc4(t3):
                a_ = t3[:]
                return bass.AP(t3, a_.offset, [list(a_.ap[0]), [16, 16], [0, 4], [1, 16]])

            def sc4(sl_):
                return bass.AP(sl_.tensor, sl_.offset, [list(sl_.ap[0]), [1, 16], [0, 4], [0, 16]])

            def cmul4(dr, di, Tdst, sr, si, xr, xi, rd):
                rd = rd + [Tu4, Tpw, Ttb]
                P.dve(lambda e: e.tensor_tensor(u4a[:], xr, sr, ALU.mult), rd, [Tu4])
                P.dve(lambda e: e.tensor_tensor(u4b[:], xi, si, ALU.mult), rd, [Tu4])
                P.dve(lambda e: e.tensor_tensor(dr[:], u4a[:], u4b[:], ALU.subtract), [Tu4], [Tdst])
                P.dve(lambda e: e.tensor_tensor(u4a[:], xi, sr, ALU.mult), rd, [Tu4])
                P.dve(lambda e: e.tensor_tensor(u4b[:], xr, si, ALU.mult), rd, [Tu4])
                P.dve(lambda e: e.tensor_tensor(di[:], u4a[:], u4b[:], ALU.add), [Tu4], [Tdst])

            def place4(dst, Tdst, src, Tsrc, scale=1.0):
                dv = dst[:].rearrange("p g (x h c) -> p g x h c", x=4, h=2)
                for h in range(2):
                    P.dve(lambda e, h=h: e.tensor_scalar(dv[h * 64:(h + 1) * 64, :, :, h, :], src[h * 64:(h + 1) * 64], scale, None, ALU.mult),
                          [Tsrc], [Tdst])
            cmul4(p4r, p4i, Tp4, pw4(4, 0), pw4(3, 0), bc4(bbr), bc4(bbi), [Tbb])
            place4(PTr, TPT, p4r, Tp4)
            place4(PTi, TPT, p4i, Tp4)
            cmul4(c4r, c4i, Tc4, sc4(pr[4]), sc4(pim[4]), p4r[:], p4i[:], [Tp4])
            place4(PPr, TPP, c4r, Tc4)
            place4(PPi, TPP, c4i, Tc4)
            cmul4(c4r, c4i, Tc4, pw4(4, 4), pw4(3, 4), bc4(crT), bc4(ciT), [TcrT, TciT])
            place4(QRf, TQ, c4r, Tc4)
            place4(QIf, TQ, c4i, Tc4, scale=-1.0)

            Tbf = s1("Tbf", [128, 16, 128], BF16); PXr = s1("PXr", [128, 16, 128], BF16)
            PXi = s1("PXi", [128, 16, 128], BF16); QRb = s1("QRb", [128, 16, 128], BF16)
            QIb = s1("QIb", [128, 16, 128], BF16)
            TTbf, TPX, TQb = T(Tbf), T(PXr), T(QRb)
            P.act(lambda e: e.activation(QRb[:], QRf[:], AF.Copy), [TQ], [TQb])
            P.act(lambda e: e.activation(QIb[:], QIf[:], AF.Copy), [TQ], [TQb])
            tmk = s1("tmk", [128, 128]); Ttmk = T(tmk)
            for gp in range(16):
                bk = nbank()

                def mmT(e, gp=gp, bk=bk):
                    e.matmul(bk.ap[:, 0:128], PTr[:, gp, :], QRf[:, gp, :], start=True, stop=False)
                    e.matmul(bk.ap[:, 0:128], PTi[:, gp, :], QIf[:, gp, :], start=False, stop=True)
                    e.transpose(bk.ap[:, 128:256], PPr[:, gp, :], identf[:])
                    return e.transpose(bk.ap[:, 256:384], PPi[:, gp, :], identf[:])
                P.pe(mmT, [TPT, TQ, TPP, Tidf], [bk])
                P.dve(lambda e, bk=bk: e.tensor_tensor(tmk[:], bk.ap[:, 0:128], maskt[:], ALU.mult), [bk, Tmaskt], [Ttmk])
                P.dve(lambda e, gp=gp: e.scalar_tensor_tensor(Tbf[:, gp, :], identf[:], dcol[:, gp:gp + 1], tmk[:], ALU.mult, ALU.add),
                      [Ttmk, Tidf, Tdcol], [TTbf])
                P.act(lambda e, gp=gp, bk=bk: e.activation(PXr[:, gp, :], bk.ap[:, 128:256], AF.Copy), [bk], [TPX])
                P.act(lambda e, gp=gp, bk=bk: e.activation(PXi[:, gp, :], bk.ap[:, 256:384], AF.Copy), [bk], [TPX])

            COS = s1("COS", [128, 16, 32]); SIN = s1("SIN", [128, 16, 32]); RHO = s1("RHO", [128, 16, 32])
            tg1 = s1("tg1", [128, 16, 32]); tg2 = s1("tg2", [128, 16, 32])
            TCOS, Ttg = T(COS), T(tg1)
            kb = bass.AP(kix, kix[:].offset, [list(kix[:].ap[0]), [0, 16], [1, 32]])
            th4 = slot()
            V(lambda e: e.tensor_scalar(th4, aid, 4.0, None, ALU.mult))
            th4b = bc_last(th4.rearrange("p (g o) -> p g o", o=1), 32)
            ard4b = bc_last(ard.rearrange("p (g o) -> p g o", o=1), 32)

            def Vt(fn):
                P.dve(fn, [Ttb, Tkix, Ttg, TCOS], [Ttg, TCOS])

            def At(fn):
                P.act(fn, [Ttb, Tkix, Ttg, TCOS], [Ttg, TCOS])

            def angk(sh):
                return [lambda e: e.tensor_tensor(tg1[:], kb, th4b, ALU.mult),
                        lambda e, sh=sh: e.tensor_scalar(tg1[:], tg1[:], sh, None, ALU.add)]
            sincos(angk, None, SIN[:], COS[:], tg1[:], tg2[:], Vt, At)
            Vt(lambda e: e.tensor_scalar(tg1[:], kb, 0.0, None, ALU.mult))
            Vt(lambda e: e.tensor_tensor(tg1[:], tg1[:], ard4b, ALU.add))
            At(lambda e: e.activation(RHO[:], tg1[:], AF.Exp, scale=4.0))
            Vt(lambda e: e.memset(RHO[:, :, 0:1], 0.0))

            h0rs = s1("h0rs", [128, 16]); h0is = s1("h0is", [128, 16]); Th0 = T(h0rs)
            P.dma(h0rs[:], h0r[:], [], [Th0])
            P.dma(h0is[:], h0i[:], [], [Th0], q="act")

            if "ssm_mats" in dbg_o:
                dm = s1("dm", [128, 4, 128]); Tdm = T(dm)
                P.act(lambda e: e.activation(dm[:, 0, :], Tbf[:, 3, :], AF.Copy), [TTbf], [Tdm])
                P.act(lambda e: e.activation(dm[:, 1, :], PXr[:, 3, :], AF.Copy), [TPX], [Tdm])
                P.act(lambda e: e.activation(dm[:, 2, :], QRb[:, 3, :], AF.Copy), [TQb], [Tdm])
                P.act(lambda e: e.activation(dm[:, 3, :], QIb[:, 3, :], AF.Copy), [TQb], [Tdm])
                P.dma(dbg_o["ssm_mats"].rearrange("(a p) n -> p a n", p=128), dm[:], [Tdm], [], is_out=True)

            P.barrier()
            if stop_after == 0:
                P.emit()
                SSbox[0].close()
                return nc
            SSbox[0].close()
            SSbox[0] = None
            es = s1("es", [64, 8]); Tes = T(es)
            P.dma(es[:], dram_bc_rows(sinks, 64), [], [Tes])
            P.act(lambda e: e.activation(es[:], es[:], AF.Exp), [Tes], [Tes])
            esk = s1("esk", [64, 2, 512]); Tesk = T(esk)
            for g in range(2):
                for hf in range(2):
                    for c in range(2):
                        h = 2 * (2 * g + c) + hf
                        col = (hf * 2 + c) * 128
                        P.dve(lambda e, g=g, h=h, col=col: e.tensor_copy(esk[:, g, col:col + 128], bc_last(es[:, h:h + 1], 128)),
                              [Tes], [Tesk])
            esrow = s1("esrow", [1, 2, 512], BF16); Tesrow = T(esrow)
            P.dve(lambda e: e.tensor_copy(esrow[:], esk[0:1, :, :]), [Tesk], [Tesrow])
            ones_b = s1("ones_b", [128, 64], BF16); Tones = T(ones_b)
            P.pool(lambda e: e.memset(ones_b[:], 1.0), [], [Tones])

            NSL = 7
            KP = [s1("KP%d" % i, [128, NSL, 128], BF16) for i in range(4)]
            Vb = s1("Vb", [128, NSL, 128], BF16)
            TkT = [T(KP[0], "kT%d" % i) for i in range(NSL)]
            TkS = [T(KP[1], "kS%d" % i) for i in range(NSL)]
            for i in range(4):
                P.pool(lambda e, i=i: e.memset(KP[i][:], 0.0), [], TkT + TkS)
            TVb = [T(Vb, "V%d" % i) for i in range(NSL)]

            def kslot(j, tt):
                return 6 if (tt == 16 and j == 15) else j % 6

            xt3 = [s1("xt%d" % i, [128, 1024]) for i in range(3)]; Txt3 = [T(t) for t in xt3]
            xn2 = [s1("xn%d" % i, [128, 1024], BF16) for i in range(2)]; Txn2 = [T(t) for t in xn2]
            junk = s1("junk", [128, 1024], BF16); Tjunk = T(junk)
            ss2 = [s1("ss%d" % i, [128, 4]) for i in range(2)]; Tss2 = [T(t) for t in ss2]
            xnT2 = [s1("xnT%d" % i, [128, 8, 128], BF16) for i in range(2)]; TxnT2 = [T(t) for t in xnT2]
            xnTp2 = [s1("xnTp%d" % i, [128, 8, 128], BF16) for i in range(2)]; TxnTp2 = [T(t) for t in xnTp2]
            qT2 = [s1("qT%d" % i, [128, 4, 128], BF16) for i in range(2)]; TqT2 = [T(t) for t in qT2]
            rt3 = [s1("rt%d" % i, [128, 16]) for i in range(3)]; Trt3 = [T(t) for t in rt3]
            qf2 = [s1("qf%d" % i, [128, 512]) for i in range(2)]; Tqf2 = [T(t) for t in qf2]
            kvf2 = [s1("kvf%d" % i, [128, 256]) for i in range(2)]; Tkvf2 = [T(t) for t in kvf2]
            qb2 = [s1("qb%d" % i, [128, 512], BF16) for i in range(2)]; Tqb2 = [T(t) for t in qb2]
            kb2 = [s1("kb%d" % i, [128, 128], BF16) for i in range(2)]; Tkb2 = [T(t) for t in kb2]
            rr = [s1("rr%d" % i, [128, 8, 8]) for i in range(4)]; Trr = T(rr[0])
            utok = s1("utok", [128, 512], BF16); Tutok = T(utok)
            Ub4 = [s1("Ub%d" % i, [128, 16, 32], BF16) for i in range(4)]; TUb4 = [T(t) for t in Ub4]
            pT2 = [s1("pT%d" % i, [128, 4, 512], BF16) for i in range(2)]; TpT2 = [T(t) for t in pT2]
            at1g = [s1("at1_%d" % i, [64, 512]) for i in range(2)]; at2g = [s1("at2_%d" % i, [64, 512]) for i in range(2)]
            Tatg = [T(t) for t in at1g]
            ckf = s1("ckf", [128, 128]); Tckf = T(ckf)
            ckb = s1("ckb", [128, 128], BF16); Tckb = T(ckb)
            zr = s1("zr", [128, 16, 32]); zi = s1("zi", [128, 16, 32]); za = s1("za", [128, 16, 32]); zb = s1("zb", [128, 16, 32])
            Tz = T(zr)
            gr = s1("gr", [128, 16, 32]); gi = s1("gi", [128, 16, 32]); Tg = T(gr)
            HXr2 = [s1("HXr%d" % i, [128, 16, 33], BF16) for i in range(2)]
            HXi2 = [s1("HXi%d" % i, [128, 16, 33], BF16) for i in range(2)]
            THX2 = [T(t) for t in HXr2]
            hz = s1("hz", [128, 16]); Thz = T(hz)
            P.pool(lambda e: e.memset(hz[:], 0.0), [], [Thz])
            hsr = s1("hsr", [128, 16]); hsi = s1("hsi", [128, 16]); Ths = T(hsr)
            ysb = s1("ysb", [128, 512], BF16); Tysb = T(ysb)
            hcr = s1("hcr", [128, 16]); hci = s1("hci", [128, 16]); Thc = T(hcr)
            ztk = s1("ztk", [128, 512], BF16); Tztk = T(ztk)
            C32, S32 = COS[:], SIN[:]
            a4r = pr[4].rearrange("p (g o) -> p g o", o=1)
            a4i = pim[4].rearrange("p (g o) -> p g o", o=1)
            fl = lambda t: t[:].rearrange("p g k -> p (g k)")

            def I_Ab(tt):
                bi = tt % 2
                norm_back(tt, xn2[bi], Txn2[bi], xnT2[bi], TxnT2[bi], 0, xnTp=xnTp2[bi], TxnTp=TxnTp2[bi])

            def I_A0(tt):
                b3 = tt % 3
                P.dma(xt3[b3][:], xall[tt * 128:(tt + 1) * 128, :], [], [Txt3[b3]])
                P.dma(rt3[b3][:], ropetok[tt * 128:(tt + 1) * 128, :], [], [Trt3[b3]])

            def I_As(tt):
                bi = tt % 2
                norm_front(tt, xt3[tt % 3], Txt3[tt % 3], xn2[bi], Txn2[bi], ss2[bi], Tss2[bi], junk, Tjunk, dma=False, part=1)

            def I_A(tt):
                bi = tt % 2
                norm_front(tt, xt3[tt % 3], Txt3[tt % 3], xn2[bi], Txn2[bi], ss2[bi], Tss2[bi], junk, Tjunk, dma=False, part=2)

            def I_B(tt):
                bi = tt % 2
                xnT, TxnT = xnT2[bi], TxnT2[bi]
                xnTp, TxnTp = xnTp2[bi], TxnTp2[bi]
                qT, TqT = qT2[bi], TqT2[bi]
                sl = kslot(tt, tt)
                bk = nbank()
                Ub, TUb = Ub4[tt % 4], TUb4[tt % 4]

                def mmu(e, bk=bk, xnTp=xnTp):
                    r = None
                    for k in range(8):
                        r = e.matmul(bk.ap[:, :], xnTp[:, k, :], wu[:, k, :], start=(k == 0), stop=(k == 7))
                    return r
                P.pe(mmu, [Twu, TxnTp], [bk])
                P.act(lambda e, bk=bk: e.activation(utok[:], bk.ap[:, :], AF.Copy), [bk], [Tutok])
                P.dve(lambda e, Ub=Ub: e.transpose(Ub[:].rearrange("p g k -> p (g k)"), utok[:]), [Tutok], [TUb])
                bq = nbank()
                bkv = nbank()

                def mmqkv(e, bq=bq, bkv=bkv, xnT=xnT):
                    r = None
                    for k in range(8):
                        e.matmul(bq.ap[:, :], xnT[:, k, :], wq[:, k, :], start=(k == 0), stop=(k == 7))
                    for k in range(8):
                        r = e.matmul(bkv.ap[:, 0:256], xnT[:, k, :], wkv[:, k, :], start=(k == 0), stop=(k == 7))
                    return r
                P.pe(mmqkv, [Twq, Twkv, TxnT], [bq, bkv])
                qf, Tqf = qf2[bi], Tqf2[bi]
                kvf, Tkvf = kvf2[bi], Tkvf2[bi]
                P.act(lambda e, bq=bq, qf=qf: e.activation(qf[:], bq.ap[:, :], AF.Copy), [bq], [Tqf])
                P.act(lambda e, bkv=bkv, kvf=kvf: e.activation(kvf[:], bkv.ap[:, 0:256], AF.Copy), [bkv], [Tkvf])
                rp, Trp = rt3[tt % 3], Trt3[tt % 3]
                for (src, Tsrc, nh, dst_b, Tdst_b, tag) in ((qf, Tqf, 8, qb2[bi], Tqb2[bi], "q"), (kvf, Tkvf, 2, kb2[bi], Tkb2[bi], "k")):
                    xv = src[:, 0:nh * 64].rearrange("p (h d) -> p h d", d=64)
                    cb = bass.AP(rp, rp[:].offset, [list(rp[:].ap[0]), [0, nh], [1, 8]])
                    sb_ = bass.AP(rp, rp[:].offset + 8, [list(rp[:].ap[0]), [0, nh], [1, 8]])
                    r1, r2, r3, r4 = (rr[i][:, 0:nh, :] for i in range(4))
                    P.pool(lambda e, xv=xv, cb=cb, r1=r1: e.tensor_tensor(r1, xv[:, :, 0:8], cb, ALU.mult), [Tsrc, Trp, Trr], [Trr])
                    P.pool(lambda e, xv=xv, sb_=sb_, r2=r2: e.tensor_tensor(r2, xv[:, :, 8:16], sb_, ALU.mult), [Tsrc, Trp, Trr], [Trr])
                    P.pool(lambda e, xv=xv, cb=cb, r3=r3: e.tensor_tensor(r3, xv[:, :, 8:16], cb, ALU.mult), [Tsrc, Trp, Trr], [Trr])
                    P.pool(lambda e, xv=xv, sb_=sb_, r4=r4: e.tensor_tensor(r4, xv[:, :, 0:8], sb_, ALU.mult), [Tsrc, Trp, Trr], [Trr])
                    P.pool(lambda e, xv=xv, r1=r1, r2=r2: e.tensor_tensor(xv[:, :, 0:8], r1, r2, ALU.subtract), [Trr], [Tsrc])
                    P.pool(lambda e, xv=xv, r3=r3, r4=r4: e.tensor_tensor(xv[:, :, 8:16], r3, r4, ALU.add), [Trr], [Tsrc])
                    P.pool(lambda e, src=src, dst_b=dst_b, nh=nh: e.tensor_copy(dst_b[:], src[:, 0:nh * 64]), [Tsrc], [Tdst_b])
                P.pool(lambda e, kvf=kvf, sl=sl: e.tensor_copy(Vb[:, sl, :], kvf[:, 128:256]), [Tkvf], [TVb[sl]])
                if tt == 15:
                    P.dma(kp_o[:], kvf[:, 0:128], [Tkvf], [], is_out=True)
                    P.dma(vp_o[:], kvf[:, 128:256], [Tkvf], [], is_out=True)
                elif tt == 16:
                    P.dma(ks_o[64:128, :], kvf[0:64, 0:128], [Tkvf], [], is_out=True)
                    P.dma(ks_o[0:64, :], ck[64:128, :], [], [], is_out=True)
                    P.dma(vs_o[64:128, :], kvf[0:64, 128:256], [Tkvf], [], is_out=True)
                    P.dma(vs_o[0:64, :], cv[64:128, :], [], [], is_out=True)

            def I_B2(tt):
                bi = tt % 2
                qT, TqT = qT2[bi], TqT2[bi]
                sl = kslot(tt, tt)
                bt = nbank()
                pbt = bt.ap[:].bitcast(BF16)
                qb, kb = qb2[bi], kb2[bi]

                def trqk(e, pbt=pbt, qb=qb, kb=kb):
                    for c in range(4):
                        e.transpose(pbt[:, c * 128:(c + 1) * 128], qb[:, c * 128:(c + 1) * 128], identb[:])
                    return e.transpose(pbt[:, 512:640], kb[:], identb[:])
                P.pe(trqk, [Tqb2[bi], Tkb2[bi], Tidb], [bt])
                P.act(lambda e, pbt=pbt, qT=qT: e.activation(qT[:], pbt[:, 0:512].rearrange("p (c t) -> p c t", c=4), AF.Copy), [bt], [TqT])
                P.act(lambda e, pbt=pbt, sl=sl: e.activation(KP[0][0:64, sl, :], pbt[0:64, 512:640], AF.Copy), [bt], [TkT[sl]])
                P.act(lambda e, pbt=pbt, sl=sl: e.activation(KP[3][64:128, sl, :], pbt[64:128, 512:640], AF.Copy), [bt], [TkT[sl]])
                P.pool(lambda e, sl=sl: e.tensor_copy(KP[1][64:128, sl, :], KP[0][0:64, sl, :]), [TkT[sl]], [TkS[sl]])
                P.pool(lambda e, sl=sl: e.tensor_copy(KP[2][0:64, sl, :], KP[3][64:128, sl, :]), [TkT[sl]], [TkS[sl]])
                if tt == 16:
                    psl = 6
                    P.dma(ckf[:], ck[:], [], [Tckf])
                    P.act(lambda e: e.activation(ckb[:], ckf[:], AF.Copy), [Tckf], [Tckb])
                    bk3 = nbank()
                    pb3 = bk3.ap[:].bitcast(BF16)
                    P.pe(lambda e: e.transpose(pb3[:, 0:128], ckb[:], identb[:]), [Tckb, Tidb], [bk3])
                    P.act(lambda e: e.activation(KP[0][0:64, psl, :], pb3[0:64, 0:128], AF.Copy), [bk3], [TkT[psl]])
                    P.act(lambda e: e.activation(KP[3][64:128, psl, :], pb3[64:128, 0:128], AF.Copy), [bk3], [TkT[psl]])
                    P.pool(lambda e: e.tensor_copy(KP[1][64:128, psl, :], KP[0][0:64, psl, :]), [TkT[psl]], [TkS[psl]])
                    P.pool(lambda e: e.tensor_copy(KP[2][0:64, psl, :], KP[3][64:128, psl, :]), [TkT[psl]], [TkS[psl]])
                    P.dma(Vb[:, psl, :], cv[:], [], [TVb[psl]], q="pool")

            def I_C(tt):
                bi = tt % 2
                qT, TqT = qT2[bi], TqT2[bi]
                pT, TpT = pT2[bi], TpT2[bi]
                Ub, TUb = Ub4[tt % 4], TUb4[tt % 4]
                jl = [tt] if tt == 0 else [tt - 1, tt]
                for g in range(2):
                    for ji, j in enumerate(jl):
                        jsl = kslot(j, tt)
                        bk = nbank()

                        def mms(e, bk=bk, g=g, jsl=jsl, qT=qT):
                            r = None
                            for hf in range(2):
                                r = e.matmul(bk.ap[:, hf * 256:(hf + 1) * 256], KP[2 * g + hf][:, jsl, :],
                                             qT[:, 2 * g:2 * g + 2, :], start=True, stop=True)
                            return r
                        P.pe(mms, [TkT[jsl], TkS[jsl], TqT], [bk])
                        slot_p = ji if len(jl) == 2 else 1
                        pg = pT[:, slot_p, :] if g == 0 else None
                        P.act(lambda e, bk=bk, slot_p=slot_p, pT=pT, g=g: e.activation(pT[:, 2 * g + slot_p, :], bk.ap[:, :], AF.Exp, scale=0.125),
                              [bk], [TpT])
                        pv = pT[:, 2 * g + slot_p, :].rearrange("p (h q) -> p h q", q=128)
                        if j == tt:
                            P.pool(lambda e, pv=pv: e.memset(pv[64:128, :, 0:64], 0.0), [TpT], [TpT])
                        else:
                            P.pool(lambda e, pv=pv: e.memset(pv[0:64, :, 64:128], 0.0), [TpT], [TpT])

            def I_Cs(tt):
                bi = tt % 2
                Ub, TUb = Ub4[tt % 4], TUb4[tt % 4]
                HXr, HXi, THX = HXr2[bi], HXi2[bi], THX2[bi]
                if tt == 16:
                    csr, csi, Tcs_ = h0rs, h0is, Th0
                elif tt > 0:
                    csr, csi, Tcs_ = hcr, hci, Thc
                else:
                    csr, csi, Tcs_ = hz, hz, Thz
                c3r = csr[:].rearrange("p (g o) -> p g o", o=1)
                c3i = csi[:].rearrange("p (g o) -> p g o", o=1)
                P.dve(lambda e, HXr=HXr, c3r=c3r: e.tensor_copy(HXr[:, :, 0:1], c3r), [Tcs_], [THX])
                P.dve(lambda e, HXi=HXi, c3i=c3i: e.tensor_copy(HXi[:, :, 0:1], c3i), [Tcs_], [THX])
                bxr = nbank()
                bxi = nbank()

                def mmx(e, bxr=bxr, bxi=bxi, Ub=Ub):
                    r = None
                    for gp in range(16):
                        e.matmul(bxr.ap[:, gp * 32:(gp + 1) * 32], PXr[:, gp, :], Ub[:, gp, :], start=True, stop=True)
                        r = e.matmul(bxi.ap[:, gp * 32:(gp + 1) * 32], PXi[:, gp, :], Ub[:, gp, :], start=True, stop=True)
                    return r
                P.pe(mmx, [TPX, TUb], [bxr, bxi])
                xr_v = bxr.ap[:, :].rearrange("p (g k) -> p g k", k=32)
                xi_v = bxi.ap[:, :].rearrange("p (g k) -> p g k", k=32)
                rdz = [bxr, bxi, TCOS, Tz]
                P.dve(lambda e, bxr=bxr: e.tensor_tensor(fl(za), bxr.ap[:, :], fl(COS), ALU.mult), rdz, [Tz])
                P.dve(lambda e, bxi=bxi: e.tensor_tensor(fl(zb), bxi.ap[:, :], fl(SIN), ALU.mult), rdz, [Tz])
                P.dve(lambda e: e.tensor_tensor(fl(zr), fl(za), fl(zb), ALU.add), [Tz], [Tz])
                P.dve(lambda e, bxi=bxi: e.tensor_tensor(fl(za), bxi.ap[:, :], fl(COS), ALU.mult), rdz, [Tz])
                P.dve(lambda e, bxr=bxr: e.tensor_tensor(fl(zb), bxr.ap[:, :], fl(SIN), ALU.mult), rdz, [Tz])
                P.dve(lambda e: e.tensor_tensor(fl(zi), fl(za), fl(zb), ALU.subtract), [Tz], [Tz])
                rdh = [Tcs_, Ttb, Tz]
                P.dve(lambda e, c3r=c3r: e.tensor_tensor(za[:, :, 0:1], c3r, a4r, ALU.mult), rdh, [Tz])
                P.dve(lambda e, c3i=c3i: e.tensor_tensor(zb[:, :, 0:1], c3i, a4i, ALU.mult), rdh, [Tz])
                P.dve(lambda e: e.tensor_tensor(zr[:, :, 0:1], zr[:, :, 0:1], za[:, :, 0:1], ALU.add), [Tz], [Tz])
                P.dve(lambda e: e.tensor_tensor(zr[:, :, 0:1], zr[:, :, 0:1], zb[:, :, 0:1], ALU.subtract), [Tz], [Tz])
                P.dve(lambda e, c3i=c3i: e.tensor_tensor(za[:, :, 0:1], c3i, a4r, ALU.mult), rdh, [Tz])
                P.dve(lambda e, c3r=c3r: e.tensor_tensor(zb[:, :, 0:1], c3r, a4i, ALU.mult), rdh, [Tz])
                P.dve(lambda e: e.tensor_tensor(zi[:, :, 0:1], zi[:, :, 0:1], za[:, :, 0:1], ALU.add), [Tz], [Tz])
                P.dve(lambda e: e.tensor_tensor(zi[:, :, 0:1], zi[:, :, 0:1], zb[:, :, 0:1], ALU.add), [Tz], [Tz])
                P.dve(lambda e: e.tensor_tensor_scan(fl(gr), fl(RHO), fl(zr), 0.0, ALU.mult, ALU.add), [Tz, TCOS], [Tg])
                P.dve(lambda e: e.tensor_tensor_scan(fl(gi), fl(RHO), fl(zi), 0.0, ALU.mult, ALU.add), [Tz, TCOS], [Tg])
                hcr3 = hcr[:].rearrange("p (g o) -> p g o", o=1)
                hci3 = hci[:].rearrange("p (g o) -> p g o", o=1)
                P.dve(lambda e: e.tensor_tensor(fl(za), fl(gr), fl(COS), ALU.mult), [Tg, TCOS, Tz], [Tz])
                P.dve(lambda e: e.tensor_tensor(fl(zb), fl(gi), fl(SIN), ALU.mult), [Tg, TCOS, Tz], [Tz])
                P.dve(lambda e, HXr=HXr: e.tensor_tensor(HXr[:, :, 1:33], za[:], zb[:], ALU.subtract), [Tz], [THX])
                P.dve(lambda e: e.tensor_tensor(hcr3, za[:, :, 31:32], zb[:, :, 31:32], ALU.subtract), [Tz, Tcs_], [Thc])
                if tt == 16:
                    P.dve(lambda e: e.tensor_tensor(hsr[:].rearrange("p (g o) -> p g o", o=1), za[:, :, 15:16], zb[:, :, 15:16], ALU.subtract), [Tz], [Ths])
                P.dve(lambda e: e.tensor_tensor(fl(za), fl(gi), fl(COS), ALU.mult), [Tg, TCOS, Tz], [Tz])
                P.dve(lambda e: e.tensor_tensor(fl(zb), fl(gr), fl(SIN), ALU.mult), [Tg, TCOS, Tz], [Tz])
                P.dve(lambda e, HXi=HXi: e.tensor_tensor(HXi[:, :, 1:33], za[:], zb[:], ALU.add), [Tz], [THX])
                P.dve(lambda e: e.tensor_tensor(hci3, za[:, :, 31:32], zb[:, :, 31:32], ALU.add), [Tz, Tcs_], [Thc])
                if tt == 16:
                    P.dve(lambda e: e.tensor_tensor(hsi[:].rearrange("p (g o) -> p g o", o=1), za[:, :, 15:16], zb[:, :, 15:16], ALU.add), [Tz], [Ths])
                if tt == min(15, ntiles - 1):
                    P.dma(srp_o[:], hcr[:], [Thc], [], is_out=True)
                    P.dma(sip_o[:], hci[:], [Thc], [], is_out=True)
                if tt == 16:
                    P.dma(srs_o[:], hsr[:], [Ths], [], is_out=True)
                    P.dma(sis_o[:], hsi[:], [Ths], [], is_out=True)

            def I_D(tt):
                bi = tt % 2
                pT, TpT = pT2[bi], TpT2[bi]
                Ub, TUb = Ub4[tt % 4], TUb4[tt % 4]
                Hbr, Hbi, THb = HXr2[bi], HXi2[bi], THX2[bi]
                jl = [tt] if tt == 0 else [tt - 1, tt]
                for g in range(2):
                    bko = nbank()
                    bkd = nbank()

                    def mmo(e, bko=bko, bkd=bkd, g=g, jl=jl, pT=pT, tt=tt):
                        r = None
                        n = len(jl)
                        for ji, j in enumerate(jl):
                            sp_ = 2 * g + (ji if n == 2 else 1)
                            e.matmul(bko.ap[0:64, :], Vb[:, kslot(j, tt), g * 64:(g + 1) * 64], pT[:, sp_, :], start=(ji == 0), stop=(ji == n - 1))
                        for ji, j in enumerate(jl):
                            sp_ = 2 * g + (ji if n == 2 else 1)
                            e.matmul(bkd.ap[0:64, :], ones_b[:], pT[:, sp_, :], start=(ji == 0), stop=False)
                        return e.matmul(bkd.ap[0:64, :], ones_b[0:1, :], esrow[0:1, g, :], start=False, stop=True)
                    P.pe(mmo, [TVb[kslot(j, tt)] for j in jl] + [TpT, Tones, Tesrow], [bko, bkd])
                    at1, at2, Tat = at1g[g], at2g[g], Tatg[g]
                    P.act(lambda e, bkd=bkd, at2=at2: e.activation(at2[:], bkd.ap[0:64, :], AF.Ln), [bkd], [Tat])
                    P.act(lambda e, at2=at2: e.activation(at2[:], at2[:], AF.Exp, scale=-1.0), [Tat], [Tat])
                    P.act(lambda e, bko=bko, at1=at1: e.activation(at1[:], bko.ap[0:64, :], AF.Copy), [bko], [Tat])
                    for hf in range(2):
                        for c in range(2):
                            col = (hf * 2 + c) * 128
                            ch = 2 * g + c
                            P.pool(lambda e, hf=hf, ch=ch, col=col, tt=tt, at1=at1, at2=at2: e.tensor_tensor(
                                attnT[hf * 64:(hf + 1) * 64, ch, tt * 128:(tt + 1) * 128], at1[:, col:col + 128], at2[:, col:col + 128], ALU.mult),
                                [Tat], [TattnT[ch][tt]])
                by = nbank()

                def mmy(e, by=by, Ub=Ub, Hbr=Hbr, Hbi=Hbi):
                    r = None
                    for gp in range(16):
                        o = by.ap[:, gp * 32:(gp + 1) * 32]
                        e.matmul(o, Tbf[:, gp, :], Ub[:, gp, :], start=True, stop=False)
                        e.matmul(o, QRb[:, gp, :], Hbr[:, gp, 0:32], start=False, stop=False)
                        r = e.matmul(o, QIb[:, gp, :], Hbi[:, gp, 0:32], start=False, stop=True)
                    return r
                P.pe(mmy, [TTbf, TQb, TUb, THb], [by])
                P.act(lambda e, by=by: e.activation(ysb[:], by.ap[:, :], AF.Gelu_apprx_tanh), [by], [Tysb])

            def I_Dt(tt):
                P.dve(lambda e: e.transpose(ztk[:], ysb[:]), [Tysb], [Tztk])

            def I_Db(tt):
                bz = nbank()
                pbz = bz.ap[:].bitcast(BF16)

                def trz(e, pbz=pbz):
                    r = None
                    for fc in range(4):
                        r = e.transpose(pbz[:, fc * 128:(fc + 1) * 128], ztk[:, fc * 128:(fc + 1) * 128], identb[:])
                    return r
                P.pe(trz, [Tztk, Tidb], [bz])
                P.act(lambda e, pbz=pbz, tt=tt: e.activation(
                    zT[:, :, tt * 128:(tt + 1) * 128].rearrange("p f (k t) -> p f t k", t=4),
                    pbz[:, 0:512].rearrange("p (f t k) -> p f t k", f=4, t=4), AF.Copy),
                    [bz], [TzT[tt]])

            def run_if(fn, t_):
                if 0 <= t_ < ntiles:
                    fn(t_)
            run_if(I_A0, 0)
            run_if(I_As, 0)
            for step in range(ntiles + 5):
                run_if(I_A0, step + 1)
                run_if(I_A, step)
                run_if(I_Cs, step - 3)
                run_if(I_D, step - 4)
                run_if(I_C, step - 3)
                run_if(I_Db, step - 5)
                run_if(I_B2, step - 2)
                run_if(I_B, step - 1)
                run_if(I_Ab, step)
                run_if(I_As, step + 1)
                run_if(I_Dt, step - 4)

            if "attn" in dbg_o:
                da = s1("da", [128, 4, 256]); Tda = T(da)
                P.act(lambda e: e.activation(da[:, :, 0:128], attnT[:, :, 5 * 128:6 * 128], AF.Copy), [TattnT[c][5] for c in range(4)], [Tda])
                P.act(lambda e: e.activation(da[:, :, 128:256], attnT[:, :, 16 * 128:17 * 128], AF.Copy), [TattnT[c][16] for c in range(4)], [Tda])
                P.dma(dbg_o["attn"].rearrange("(c p) n -> p c n", p=128), da[:], [Tda], [], is_out=True)
            if "zT" in dbg_o:
                dz = s1("dz", [128, 4, 256]); Tdz = T(dz)
                P.act(lambda e: e.activation(dz[:, :, 0:128], zT[:, :, 5 * 128:6 * 128], AF.Copy), [TzT[5]], [Tdz])
                P.act(lambda e: e.activation(dz[:, :, 128:256], zT[:, :, 16 * 128:17 * 128], AF.Copy), [TzT[16]], [Tdz])
                P.dma(dbg_o["zT"].rearrange("(c p) n -> p c n", p=128), dz[:], [Tdz], [], is_out=True)
            P.barrier()

        if stop_after == 1:
            P.emit()
            return nc

        def mgap(k, tc):
            return attnT[:, k, tc] if k < 4 else zT[:, k - 4, tc]

        def Tmg_all(tt):
            return [TattnT[c][tt] for c in range(4)] + [TzT[tt]]
        wo = sb("wo", [128, 8, 1024], BF16); Two = T(wo)
        wp = sb("wp", [128, 2, 1024], BF16); Twp = T(wp)

        def prefetch_iib():
            P.dma(wo[:], w_out.rearrange("(k p) n -> p k n", p=128), [], [Two], q="pool")
            P.dma(wp[:], w_pp.rearrange("(k p) n -> p k n", p=128), [], [Twp], q="pool")
        with ExitStack() as S2:
            def s2(name, shape, dt=F32):
                return sb(name, shape, dt, S2)

            def wload2(name, src, kch, ncols):
                t = s2(name, [128, kch, ncols], BF16)
                tt_ = T(t, name)
                P.dma(t[:], src.rearrange("(k p) n -> p k n", p=128), [], [tt_], q="pool")
                return t, tt_
            wgl, Twgl = wload2("wgl", w_glu, 4, 512)
            wza, Twza = wload2("wza", w_in[:, 768:1280], 8, 512)
            wzb, Twzb = wload2("wzb", w_in[:, 1792:2304], 8, 512)
            wga, Twga = wload2("wga", w_in[:, 2304:3328], 8, 1024)
            wgb, Twgb = wload2("wgb", w_in[:, 3328:4352], 8, 1024)
            woa, Twoa = wload2("woa", w_oa, 4, 1024)
            wos, Twos = wload2("wos", w_os, 4, 1024)
            xt2 = [s2("xtb%d" % i, [128, 1024]) for i in range(2)]; Txt2 = [T(t) for t in xt2]
            xn2 = [s2("xnb%d" % i, [128, 1024], BF16) for i in range(2)]; Txn2 = [T(t) for t in xn2]
            junk = s2("junkb", [128, 1024], BF16); Tjunk = T(junk)
            ss2 = [s2("ssb%d" % i, [128, 4]) for i in range(2)]; Tss2 = [T(t) for t in ss2]
            xnTb2 = [s2("xnTb%d" % i, [128, 8, 512], BF16) for i in range(2)]; TxnTb2 = [T(t) for t in xnTb2]
            sla = s2("sla", [128, 4, 512], BF16); Tsla = T(sla)
            slb = s2("slb", [128, 4, 512], BF16); Tslb = T(slb)
            sga = s2("sga", [128, 8, 512], BF16); Tsga = T(sga)
            sgb = s2("sgb", [128, 8, 512], BF16); Tsgb = T(sgb)
            ag = s2("ag", [128, 4, 512], BF16); Tag = T(ag)
            gl = s2("gl", [128, 4, 512], BF16); Tgl = T(gl)
            sg = s2("sg", [128, 4, 512], BF16); Tsg = T(sg)
            m1b = [s2("m1_%d" % i, [128, 512]) for i in range(2)]; m2b = [s2("m2_%d" % i, [128, 512]) for i in range(2)]
            Tm1b = [T(t) for t in m1b]; Tm2b = [T(t) for t in m2b]
            blocks = [(0, 4), (4, 8), (8, 12), (12, 16), (16, 17)]

            def A_A(b):
                t0, t1 = blocks[b]
                for tt in range(t0, t1):
                    bi = tt % 2
                    load_norm_tile(tt, xt2[bi], Txt2[bi], xn2[bi], Txn2[bi], xnTb2[b % 2], TxnTb2[b % 2], (tt - t0) * 128,
                                   ss2[bi], Tss2[bi], junk, Tjunk)

            gcount = [0]

            def A_B(b):
                t0, t1 = blocks[b]
                N = (t1 - t0) * 128
                xnT, TxnT = xnTb2[b % 2], TxnTb2[b % 2]
                nxt = list(range(*blocks[b + 1])) if b + 1 < len(blocks) else []
                n0 = blocks[b + 1][0] if nxt else 0
                gcount[0] = 0

                def hook_pre():
                    gi = gcount[0]
                    if gi % 6 == 0 and gi // 6 < len(nxt):
                        tt = nxt[gi // 6]
                        bi = tt % 2
                        norm_front(tt, xt2[bi], Txt2[bi], xn2[bi], Txn2[bi], ss2[bi], Tss2[bi], junk, Tjunk)

                def hook_post():
                    gi = gcount[0]
                    if gi % 6 == 5 and gi // 6 < len(nxt):
                        tt = nxt[gi // 6]
                        bi = tt % 2
                        norm_back(tt, xn2[bi], Txn2[bi], xnTb2[(b + 1) % 2], TxnTb2[(b + 1) % 2], (tt - n0) * 128)
                    gcount[0] = gi + 1

                def proj(wt, Twt, ncol_chunks, func, dst, Tdst):
                    for c in range(ncol_chunks):
                        hook_pre()
                        bk = nbank()

                        def mm(e, bk=bk, c=c, wt=wt, xnT=xnT, N=N):
                            r = None
                            for k in range(8):
                                r = e.matmul(bk.ap[:, 0:N], wt[:, k, c * 128:(c + 1) * 128], xnT[:, k, 0:N], start=(k == 0), stop=(k == 7))
                            return r
                        P.pe(mm, [Twt, TxnT], [bk])
                        P.act(lambda e, bk=bk, c=c, dst=dst, func=func, N=N: e.activation(dst[:, c, 0:N], bk.ap[:, 0:N], func), [bk], [Tdst])
                        hook_post()
                bc = slice(t0 * 128, t1 * 128)
                rd_at = [TattnT[c][tt] for c in range(4) for tt in range(t0, t1)]
                rd_z = [TzT[tt] for tt in range(t0, t1)]
                A_C(b)
                proj(wza, Twza, 4, AF.Silu, sla, Tsla)
                P.dve(lambda e, bc=bc, N=N: e.tensor_tensor(ag[:, :, 0:N], attnT[:, :, bc], sla[:, :, 0:N], ALU.mult), rd_at + [Tsla], [Tag])
                proj(wzb, Twzb, 4, AF.Silu, slb, Tslb)
                P.dve(lambda e, bc=bc, N=N: e.tensor_tensor(sg[:, :, 0:N], zT[:, :, bc], gl[:, :, 0:N], ALU.mult), rd_z + [Tgl], [Tsg])
                P.pool(lambda e, N=N: e.tensor_tensor(sg[:, :, 0:N], sg[:, :, 0:N], slb[:, :, 0:N], ALU.mult), [Tsg, Tslb], [Tsg])
                proj(wga, Twga, 8, AF.Sigmoid, sga, Tsga)
                proj(wgb, Twgb, 8, AF.Sigmoid, sgb, Tsgb)

            def A_C(b):
                t0, t1 = blocks[b]
                N = (t1 - t0) * 128
                bc = slice(t0 * 128, t1 * 128)
                rd_z = [TzT[tt] for tt in range(t0, t1)]
                for fo in range(4):
                    bk = nbank()

                    def mmg(e, bk=bk, fo=fo, bc=bc, N=N):
                        r = None
                        for fi in range(4):
                            r = e.matmul(bk.ap[:, 0:N], wgl[:, fi, fo * 128:(fo + 1) * 128], zT[:, fi, bc], start=(fi == 0), stop=(fi == 3))
                        return r
                    P.pe(mmg, [Twgl] + rd_z, [bk])
                    P.act(lambda e, bk=bk, fo=fo, N=N: e.activation(gl[:, fo, 0:N], bk.ap[:, 0:N], AF.Sigmoid), [bk], [Tgl])

            def A_D(b):
                t0, t1 = blocks[b]
                N = (t1 - t0) * 128
                bc = slice(t0 * 128, t1 * 128)
                for mo in range(8):
                    bka = nbank()
                    bks = nbank()

                    def mmo2(e, bka=bka, bks=bks, mo=mo, N=N):
                        r = None
                        for fi in range(4):
                            e.matmul(bka.ap[:, 0:N], woa[:, fi, mo * 128:(mo + 1) * 128], ag[:, fi, 0:N], start=(fi == 0), stop=(fi == 3))
                        for fi in range(4):
                            r = e.matmul(bks.ap[:, 0:N], wos[:, fi, mo * 128:(mo + 1) * 128], sg[:, fi, 0:N], start=(fi == 0), stop=(fi == 3))
                        return r
                    P.pe(mmo2, [Twoa, Twos, Tag, Tsg], [bka, bks])
                    m1, m2, Tm1, Tm2 = m1b[mo % 2], m2b[mo % 2], Tm1b[mo % 2], Tm2b[mo % 2]
                    P.dve(lambda e, bka=bka, mo=mo, N=N, m1=m1: e.tensor_tensor(m1[:, 0:N], bka.ap[:, 0:N], sga[:, mo, 0:N], ALU.mult), [bka, Tsga], [Tm1])
                    P.dve(lambda e, bks=bks, mo=mo, N=N, m2=m2: e.tensor_tensor(m2[:, 0:N], bks.ap[:, 0:N], sgb[:, mo, 0:N], ALU.mult), [bks, Tsgb], [Tm2])
                    wr = [TattnT[mo][tt] for tt in range(t0, t1)] if mo < 4 else [TzT[tt] for tt in range(t0, t1)]
                    P.pool(lambda e, mo=mo, bc=bc, N=N, m1=m1, m2=m2: e.tensor_tensor(mgap(mo, bc), m1[:, 0:N], m2[:, 0:N], ALU.add), [Tm1, Tm2], wr)

            A_A(0)
            for b in range(len(blocks)):
                if b == 2:
                    prefetch_iib()
                A_B(b)
                A_D(b)
            P.barrier()

        with ExitStack() as S3:
            def s3(name, shape, dt=F32):
                return sb(name, shape, dt, S3)

            def wload3(name, src, kch, ncols):
                t = s3(name, [128, kch, ncols], BF16)
                tt_ = T(t, name)
                P.dma(t[:], src.rearrange("(k p) n -> p k n", p=128), [], [tt_], q="pool")
                return t, tt_
            wg, Twg = wload3("wg", w_pg, 8, 1024)
            fg_bc = s3("fg_bc", [128, 1024]); Tfg = T(fg_bc)
            P.dma(fg_bc[:], dram_bc_rows(fgain, 128), [], [Tfg])
            xt2 = [s3("xtc%d" % i, [128, 1024]) for i in range(2)]; Txt2 = [T(t) for t in xt2]
            pt2 = [s3("ptc%d" % i, [128, 256]) for i in range(2)]; Tpt2 = [T(t) for t in pt2]
            ptb2 = [s3("ptb%d" % i, [128, 256], BF16) for i in range(2)]; Tptb2 = [T(t) for t in ptb2]
            pTt2 = [s3("pTt%d" % i, [128, 2, 128], BF16) for i in range(2)]; TpTt2 = [T(t) for t in pTt2]
            h4 = [s3("h%d" % i, [128, 1024]) for i in range(4)]; Th4 = [T(t) for t in h4]
            hb2 = [s3("hb%d" % i, [128, 1024], BF16) for i in range(2)]; Thb2 = [T(t) for t in hb2]
            hT2 = [s3("hT%d" % i, [128, 8, 128], BF16) for i in range(2)]; ThT2 = [T(t) for t in hT2]
            g2t2 = [s3("g2t%d" % i, [128, 1024]) for i in range(2)]; Tg22 = [T(t) for t in g2t2]
            h22 = [s3("h2%d" % i, [128, 1024]) for i in range(2)]; Th22 = [T(t) for t in h22]
            junk = s3("junkc", [128, 1024]); Tjunk = T(junk)
            ssc2 = [s3("ssc%d" % i, [128, 4]) for i in range(2)]; Tssc2 = [T(t) for t in ssc2]
            yo2 = [s3("yo%d" % i, [128, 1024]) for i in range(2)]; Tyo2 = [T(t) for t in yo2]

            def B_A(tt):
                bi = tt % 2
                xt, Txt = xt2[bi], Txt2[bi]
                pt, Tpt = pt2[bi], Tpt2[bi]
                h, Th = h4[tt % 4], Th4[tt % 4]
                hb, Thb = hb2[bi], Thb2[bi]
                tc = slice(tt * 128, (tt + 1) * 128)
                P.dma(xt[:], xall[tc, :], [], [Txt])
                P.dma(pt[:], pall[tc, :], [], [Tpt])
                bh = [nbank(), nbank()]

                def mmh(e, bh=bh, tc=tc):
                    r = None
                    for half in range(2):
                        for k in range(8):
                            r = e.matmul(bh[half].ap[:, :], mgap(k, tc), wo[:, k, half * 512:(half + 1) * 512], start=(k == 0), stop=(k == 7))
                    return r
                P.pe(mmh, Tmg_all(tt) + [Two], bh)
                for half in range(2):
                    P.dve(lambda e, half=half, bh=bh, xt=xt, h=h: e.tensor_tensor(h[:, half * 512:(half + 1) * 512], bh[half].ap[:, :], xt[:, half * 512:(half + 1) * 512], ALU.add),
                          [bh[half], Txt], [Th])

            def B_A2(tt):
                bi = tt % 2
                pt, Tpt = pt2[bi], Tpt2[bi]
                h, Th = h4[tt % 4], Th4[tt % 4]
                hb, Thb = hb2[bi], Thb2[bi]
                P.act(lambda e, hb=hb, h=h: e.activation(hb[:], h[:], AF.Copy), [Th], [Thb])
                P.act(lambda e, pt=pt, ptb=ptb2[bi]: e.activation(ptb[:], pt[:], AF.Copy), [Tpt], [Tptb2[bi]])

            def B_B(tt):
                bi = tt % 2
                hb, Thb = hb2[bi], Thb2[bi]
                hT, ThT = hT2[bi], ThT2[bi]
                ptb, Tptb = ptb2[bi], Tptb2[bi]
                pTt, TpTt = pTt2[bi], TpTt2[bi]
                bk = nbank()
                pb = bk.ap[:].bitcast(BF16)

                def trh(e, pb=pb, hb=hb):
                    r = None
                    for k in range(8):
                        r = e.transpose(pb[:, k * 128:(k + 1) * 128], hb[:, k * 128:(k + 1) * 128], identb[:])
                    return r
                P.pe(trh, [Thb, Tidb], [bk])
                P.act(lambda e, pb=pb, hT=hT: e.activation(hT[:], pb[:, 0:1024].rearrange("p (k t) -> p k t", k=8), AF.Copy), [bk], [ThT])
                bk2 = nbank()
                pb2 = bk2.ap[:].bitcast(BF16)

                def trp(e, pb2=pb2, ptb=ptb):
                    r = None
                    for k in range(2):
                        r = e.transpose(pb2[:, k * 128:(k + 1) * 128], ptb[:, k * 128:(k + 1) * 128], identb[:])
                    return r
                P.pe(trp, [Tptb, Tidb], [bk2])
                P.act(lambda e, pb2=pb2, pTt=pTt: e.activation(pTt[:], pb2[:, 0:256].rearrange("p (k t) -> p k t", k=2), AF.Copy), [bk2], [TpTt])

            def B_C(tt):
                bi = tt % 2
                hT, ThT = hT2[bi], ThT2[bi]
                pTt, TpTt = pTt2[bi], TpTt2[bi]
                g2t, Tg2 = g2t2[bi], Tg22[bi]
                h2, Th2 = h22[bi], Th22[bi]
                h, Th = h4[tt % 4], Th4[tt % 4]
                bg = [nbank(), nbank()]

                def mmg2(e, bg=bg, hT=hT):
                    r = None
                    for half in range(2):
                        for k in range(8):
                            r = e.matmul(bg[half].ap[:, :], hT[:, k, :], wg[:, k, half * 512:(half + 1) * 512], start=(k == 0), stop=(k == 7))
                    return r
                P.pe(mmg2, [ThT, Twg], bg)
                for half in range(2):
                    P.act(lambda e, half=half, bg=bg, g2t=g2t: e.activation(g2t[:, half * 512:(half + 1) * 512], bg[half].ap[:, :], AF.Sigmoid), [bg[half]], [Tg2])
                bp = [nbank(), nbank()]

                def mmp(e, bp=bp, pTt=pTt):
                    r = None
                    for half in range(2):
                        for k in range(2):
                            r = e.matmul(bp[half].ap[:, :], pTt[:, k, :], wp[:, k, half * 512:(half + 1) * 512], start=(k == 0), stop=(k == 1))
                    return r
                P.pe(mmp, [TpTt, Twp], bp)
                for half in range(2):
                    hs = slice(half * 512, (half + 1) * 512)
                    P.dve(lambda e, half=half, bp=bp, hs=hs, h2=h2, g2t=g2t: e.tensor_tensor(h2[:, hs], bp[half].ap[:, :], g2t[:, hs], ALU.mult), [bp[half], Tg2], [Th2])
                P.pool(lambda e, h2=h2, h=h: e.tensor_tensor(h2[:], h2[:], h[:], ALU.add), [Th2, Th], [Th2])

            def B_D(tt):
                bi = tt % 2
                h2, Th2 = h22[bi], Th22[bi]
                ssc, Tssc = ssc2[bi], Tssc2[bi]
                yo, Tyo = yo2[bi], Tyo2[bi]
                tc = slice(tt * 128, (tt + 1) * 128)
                P.act(lambda e, h2=h2, ssc=ssc: e.activation(junk[:], h2[:], AF.Square, scale=1.0 / 32, accum_out=ssc[:, 0:1]), [Th2], [Tjunk, Tssc])
                P.act(lambda e, ssc=ssc: e.activation(ssc[:, 3:4], ssc[:, 0:1], AF.Ln, bias=EPS), [Tssc], [Tssc])
                P.act(lambda e, ssc=ssc: e.activation(ssc[:, 2:3], ssc[:, 3:4], AF.Exp, scale=-0.5), [Tssc], [Tssc])
                P.dve(lambda e, yo=yo, h2=h2, ssc=ssc: e.scalar_tensor_tensor(yo[:], h2[:], ssc[:, 2:3], fg_bc[:], ALU.mult, ALU.mult), [Th2, Tssc, Tfg], [Tyo])
                P.dma(y_o[tc, :], yo[:], [Tyo], [], is_out=True)

            def run3(fn, t_):
                if 0 <= t_ < NT:
                    fn(t_)
            for step in range(NT + 4):
                run3(B_C, step - 3)
                run3(B_B, step - 2)
                run3(B_A2, step - 1)
                run3(B_A, step)
                run3(B_D, step - 4)
        P.emit()
    return nc


def _consts():
    f32 = np.float32
    half = 8
    inv = np.power(np.float32(500000.0), -(np.arange(half, dtype=f32) * np.float32(2.0) / np.float32(16))).astype(f32)
    pos = np.zeros(NTOK, f32)
    pos[:2048] = np.arange(2048, dtype=f32)
    pos[2048:2048 + 64] = 1024 + np.arange(64, dtype=f32)
    ang = (pos[:, None] * inv[None, :]).astype(f32)
    cos = np.cos(ang).astype(f32)
    sin = np.sin(ang).astype(f32)
    ropetok = np.concatenate([cos, sin], axis=1).astype(f32)
    maskT = np.zeros((128, 128), f32)
    for s in range(4):
        for t in range(4):
            if t >= s:
                for g2 in range(2):
                    r0 = s * 32 + g2 * 16
                    c0 = t * 32 + g2 * 16
                    maskT[r0:r0 + 16, c0:c0 + 16] = 1.0
    return dict(ident_b=np.eye(128, dtype=f32).astype(ml_dtypes.bfloat16), ident_f=np.eye(128, dtype=f32),
                ropetok=ropetok, maskT=maskT, kidx=np.arange(64, dtype=f32),
                jtab=np.array([-1, -2, -3, -4, 1, 2, 3, 4], dtype=f32))


def _lg(a):
    return np.ascontiguousarray(np.asarray(a, np.float32).reshape(16, 2, 64).transpose(1, 2, 0).reshape(128, 16))


def _lg3(a):
    return np.ascontiguousarray(np.asarray(a, np.float32).reshape(16, 2, 64, 16).transpose(1, 2, 0, 3).reshape(128, 256))


def _ulg(a):
    return np.ascontiguousarray(np.asarray(a).reshape(2, 64, 16).transpose(2, 0, 1).reshape(32, 64))


def make_in_maps(inp):
    f32 = np.float32
    c = _consts()
    maps = []
    for b in range(8):
        xall = np.zeros((NTOK, 1024), f32)
        xall[:2048] = inp["x_prompt"][b]
        xall[2048:2112] = inp["x_sample"][b]
        pall = np.zeros((NTOK, 256), f32)
        pall[:2048] = inp["p_prompt"][0, b]
        pall[2048:2112] = inp["p_sample"][0, b]
        m = dict(c)
        m.update(
            xall=xall, pall=pall,
            ck=np.ascontiguousarray(inp["cache_attn_k"][0, b].reshape(128, 128)),
            cv=np.ascontiguousarray(inp["cache_attn_v"][0, b].reshape(128, 128)),
            h0r=_lg(inp["state_ssm_re"][0, b]),
            h0i=_lg(inp["state_ssm_im"][0, b]),
            norm_gain=np.ascontiguousarray(inp["norm_gain"][0]),
            fgain=np.ascontiguousarray(inp["final_norm_gain"]),
            w_in=np.ascontiguousarray(inp["w_in"][0]),
            sinks=np.ascontiguousarray(inp["attn_sinks"][0]),
            w_oa=np.ascontiguousarray(inp["w_o_attn"][0]),
            a_re=_lg(inp["ssm_a_re"][0]),
            a_im=_lg(inp["ssm_a_im"][0]),
            log_dt=_lg(np.repeat(inp["ssm_log_dt"][0][:, None], 64, axis=1)),
            b_re=_lg3(inp["ssm_b_re"][0]),
            b_im=_lg3(inp["ssm_b_im"][0]),
            c_re=_lg3(inp["ssm_c_re"][0].transpose(0, 2, 1)),
            c_im=_lg3(inp["ssm_c_im"][0].transpose(0, 2, 1)),
            ssm_d=np.ascontiguousarray(np.tile(inp["ssm_d"][0].reshape(16, 32).T, (4, 1))),
            w_glu=np.ascontiguousarray(inp["ssm_w_glu"][0]),
            w_os=np.ascontiguousarray(inp["w_o_ssm"][0]),
            w_out=np.ascontiguousarray(inp["w_out"][0]),
            w_pg=np.ascontiguousarray(inp["w_ple_gate"][0]),
            w_pp=np.ascontiguousarray(inp["w_ple_proj"][0]),
        )
        maps.append(m)
    return maps


_NC_CACHE = {}


def kernel(**inputs):
    inp = {k: np.asarray(v) for k, v in inputs.items()}
    if "nc" not in _NC_CACHE:
        _NC_CACHE["nc"] = build()
    nc = _NC_CACHE["nc"]
    maps = make_in_maps(inp)
    res = run_bass_kernel_spmd(nc, maps, core_ids=list(range(8)))
    R = res.results
    f32 = np.float32
    y_p = np.stack([R[b]["y"][:2048] for b in range(8)]).astype(f32)
    y_s = np.stack([R[b]["y"][2048:2112] for b in range(8)]).astype(f32)

    def st(name, shape):
        if shape == (32, 64):
            return np.stack([_ulg(R[b][name]) for b in range(8)])[None].astype(f32)
        return np.stack([R[b][name].reshape(shape) for b in range(8)])[None].astype(f32)
    return (y_p, y_s,
            st("kp", (128, 2, 64)), st("vp", (128, 2, 64)), st("srp", (32, 64)), st("sip", (32, 64)),
            st("ks", (128, 2, 64)), st("vs", (128, 2, 64)), st("srs", (32, 64)), st("sis", (32, 64)))
```

```python
import math
from contextlib import ExitStack

import numpy as np
import ml_dtypes
import concourse.bass as bass
import concourse.mybir as mybir
from concourse.bass_utils import run_bass_kernel_spmd

F32 = mybir.dt.float32
BF16 = mybir.dt.bfloat16
AF = mybir.ActivationFunctionType
ALU = mybir.AluOpType

NT = 17
NTOK = NT * 128
EPS = 1e-6
PI = math.pi
DEBUG = {}
ATTACH_WAITS = True
VCLOCK = True


class T:
    __slots__ = ("ap", "name", "lw", "rs")

    def __init__(self, ap, name=""):
        self.ap = ap
        self.name = name
        self.lw = None
        self.rs = []

    def __getitem__(self, k):
        return self.ap[k]


class Op:
    __slots__ = ("eng", "fn", "idx", "deps", "dsem", "dval", "isdma", "name", "bar", "gid", "waits", "know")


class Prog:
    ENGS = ("pe", "act", "dve", "pool", "sp")

    def __init__(self, nc, n_dma_sems=48):
        self.nc = nc
        self.ops = {e: [] for e in self.ENGS}
        self.n_dma_sems = n_dma_sems
        self.dma_rr = 0
        self.dma_rr_sw = 0
        self.dma_cnt = [0] * n_dma_sems
        self.out_dmas = []
        self.gcount = 0
        self.ccnt = {e: 0 for e in self.ENGS}

    def _add(self, eng, fn, reads, writes, isdma=False, name=""):
        op = Op()
        op.eng = eng
        op.fn = fn
        op.isdma = isdma
        op.name = name
        op.bar = None
        deps = []
        for t in reads:
            if t.lw is not None:
                deps.append(t.lw)
        for t in writes:
            if t.lw is not None:
                deps.append(t.lw)
            deps.extend(t.rs)
        for t in writes:
            t.lw = op
            t.rs = []
        for t in reads:
            if t.lw is not op:
                t.rs.append(op)
        op.deps = deps
        op.gid = self.gcount
        self.gcount += 1
        self.ops[eng].append(op)
        if not isdma:
            self.ccnt[eng] += 1
        op.idx = self.ccnt[eng]
        if isdma:
            nsw = self.n_dma_sems // 3
            if eng == "pool":
                s = self.dma_rr_sw
                self.dma_rr_sw = (self.dma_rr_sw + 1) % nsw
            else:
                s = nsw + self.dma_rr
                self.dma_rr = (self.dma_rr + 1) % (self.n_dma_sems - nsw)
            self.dma_cnt[s] += 16
            op.dsem = s
            op.dval = self.dma_cnt[s]
        return op

    def pe(self, fn, reads, writes, name=""):
        return self._add("pe", fn, reads, writes, name=name)

    def act(self, fn, reads, writes, name=""):
        return self._add("act", fn, reads, writes, name=name)

    def dve(self, fn, reads, writes, name=""):
        return self._add("dve", fn, reads, writes, name=name)

    def pool(self, fn, reads, writes, name=""):
        return self._add("pool", fn, reads, writes, name=name)

    def dma(self, out, in_, reads, writes, q="sp", is_out=False, slow=False, name=""):
        def fn(e):
            if slow:
                return e.dma_start(out=out, in_=in_, allow_slow_non_contiguous=True)
            return e.dma_start(out=out, in_=in_)
        op = self._add(q, fn, reads, writes, isdma=True, name=name)
        if is_out:
            self.out_dmas.append(op)
        return op

    def barrier(self):
        snap_c = dict(self.ccnt)
        snap_d = list(self.dma_cnt)
        for e in self.ENGS:
            op = Op()
            op.eng = e
            op.fn = None
            op.isdma = False
            op.bar = (snap_c, snap_d)
            op.deps = []
            op.idx = self.ccnt[e]
            op.name = "barrier"
            op.gid = self.gcount
            self.gcount += 1
            self.ops[e].append(op)

    def emit(self):
        nc = self.nc
        with ExitStack() as st:
            esem = {e: st.enter_context(nc.semaphore("s_" + e)) for e in self.ENGS}
            dsem = [st.enter_context(nc.semaphore("d%d" % i)) for i in range(self.n_dma_sems)]
            block = st.enter_context(nc.Block())
            prog = self

            if VCLOCK:
                known = {en: {} for en in prog.ENGS}
                allops = sorted((o for en in prog.ENGS for o in prog.ops[en]), key=lambda o: o.gid)
                for o in allops:
                    K = known[o.eng]
                    need_ = {}
                    if o.bar is not None:
                        snap_c, snap_d = o.bar
                        for en in prog.ENGS:
                            if en != o.eng and en != "sp" and K.get(("e", en), 0) < snap_c[en]:
                                need_[("e", en)] = snap_c[en]
                        for i in range(prog.n_dma_sems):
                            if K.get(("d", i), 0) < snap_d[i]:
                                need_[("d", i)] = snap_d[i]
                        for k_, v_ in need_.items():
                            K[k_] = v_
                        o.waits = list(need_.items())
                        o.know = dict(K)
                        continue
                    for d in o.deps:
                        if d.isdma:
                            key, val = ("d", d.dsem), d.dval
                        else:
                            if d.eng == "pe" and o.eng == "pe":
                                continue
                            key, val = ("e", d.eng), d.idx
                        if K.get(key, 0) >= val:
                            continue
                        K[key] = val
                        need_[key] = max(need_.get(key, 0), val)
                        if not d.isdma and d.know is not None:
                            for k2, v2 in d.know.items():
                                if K.get(k2, 0) < v2:
                                    K[k2] = v2
                    o.waits = [(k_, v_) for k_, v_ in need_.items()]
                    o.know = None if o.isdma else dict(K)

            def run(ename, e):
                waited = {}

                def w(sem, val):
                    if val <= 0:
                        return
                    k = sem.name
                    if waited.get(k, 0) >= val:
                        return
                    waited[k] = val
                    e.wait_ge(sem, val)

                for op in prog.ops[ename]:
                    if op.bar is not None and VCLOCK:
                        for (kind, ident), val in op.waits:
                            e.wait_ge(dsem[ident] if kind == "d" else esem[ident], val)
                        continue
                    if op.bar is not None:
                        snap_c, snap_d = op.bar
                        for en in prog.ENGS:
                            if en != ename and en != "sp":
                                w(esem[en], snap_c[en])
                        for i in range(prog.n_dma_sems):
                            w(dsem[i], snap_d[i])
                        continue
                    pend = []

                    def need(sem, val):
                        if val <= 0:
                            return
                        k = sem.name
                        if waited.get(k, 0) >= val:
                            return
                        waited[k] = val
                        pend.append((sem, val))
                    if VCLOCK:
                        for (kind, ident), val in op.waits:
                            pend.append((dsem[ident] if kind == "d" else esem[ident], val))
                    else:
                        for d in op.deps:
                            if d.isdma:
                                need(dsem[d.dsem], d.dval)
                            else:
                                if d.eng == "pe" and ename == "pe":
                                    continue
                                need(esem[d.eng], d.idx)
                    best = {}
                    for sem, val in pend:
                        if sem.name not in best or best[sem.name][1] < val:
                            best[sem.name] = (sem, val)
                    pend = list(best.values())
                    attach = None
                    if ATTACH_WAITS and (not op.isdma) and ename in ("act", "dve", "pool") and pend:
                        attach = pend.pop()
                    for sem, val in pend:
                        e.wait_ge(sem, val)
                    if op.isdma:
                        w(dsem[op.dsem], op.dval - 16)
                        ins = op.fn(e)
                        ins.then_inc(dsem[op.dsem], 16)
                    else:
                        ins = op.fn(e)
                        if attach is not None:
                            ins = ins._wait_ge(attach[0], attach[1])
                        ins.then_inc(esem[ename], 1)
                if ename == "sp":
                    for op in prog.out_dmas:
                        w(dsem[op.dsem], op.dval)

            @block.sync
            def _(e):
                run("sp", e)

            @block.tensor
            def _(e):
                run("pe", e)

            @block.scalar
            def _(e):
                run("act", e)

            @block.vector
            def _(e):
                run("dve", e)

            @block.gpsimd
            def _(e):
                run("pool", e)


def bc_last(ap, n):
    dims = [list(d) for d in ap.ap]
    assert dims[-1][1] == 1
    dims[-1] = [0, n]
    return bass.AP(ap.tensor, ap.offset, dims)


def dram_bc_rows(ap1d, nparts):
    dims = [list(d) for d in ap1d.ap]
    return bass.AP(ap1d.tensor, ap1d.offset, [[0, nparts]] + dims)


class _Stop(Exception):
    pass


def build(stop_after=None, dbg=(), ntiles=NT, kstop=0):
    try:
        return _build(stop_after, dbg, ntiles, kstop)
    except _Stop as ex:
        P, nc = ex.args
        P.emit()
        return nc


def _build(stop_after, dbg, ntiles, kstop):
    nc = bass.Bass("TRN2", target_bir_lowering=False)
    P = Prog(nc)

    def din(name, shape, dt=F32):
        return nc.dram_tensor(name, list(shape), dt, kind="ExternalInput").ap()

    def dout(name, shape, dt=F32):
        return nc.dram_tensor(name, list(shape), dt, kind="ExternalOutput").ap()

    xall = din("xall", [NTOK, 1024])
    pall = din("pall", [NTOK, 256])
    ck = din("ck", [128, 128])
    cv = din("cv", [128, 128])
    h0r = din("h0r", [128, 16])
    h0i = din("h0i", [128, 16])
    norm_gain = din("norm_gain", [1024])
    fgain = din("fgain", [1024])
    w_in = din("w_in", [1024, 4352])
    sinks = din("sinks", [8])
    w_oa = din("w_oa", [512, 1024])
    a_re = din("a_re", [128, 16])
    a_im = din("a_im", [128, 16])
    log_dt = din("log_dt", [128, 16])
    b_re = din("b_re", [128, 256])
    b_im = din("b_im", [128, 256])
    c_re = din("c_re", [128, 256])
    c_im = din("c_im", [128, 256])
    ssm_d = din("ssm_d", [128, 16])
    w_glu = din("w_glu", [512, 512])
    w_os = din("w_os", [512, 1024])
    w_out = din("w_out", [1024, 1024])
    w_pg = din("w_pg", [1024, 1024])
    w_pp = din("w_pp", [256, 1024])
    ident_b = din("ident_b", [128, 128], BF16)
    ident_f = din("ident_f", [128, 128])
    ropetok = din("ropetok", [NTOK, 16])
    maskT = din("maskT", [128, 128])
    kidx = din("kidx", [64])
    jtab = din("jtab", [8])

    y_o = dout("y", [NTOK, 1024])
    kp_o = dout("kp", [128, 128])
    vp_o = dout("vp", [128, 128])
    srp_o = dout("srp", [128, 16])
    sip_o = dout("sip", [128, 16])
    ks_o = dout("ks", [128, 128])
    vs_o = dout("vs", [128, 128])
    srs_o = dout("srs", [128, 16])
    sis_o = dout("sis", [128, 16])
    dbg_o = {}
    for name, shape in dbg:
        dbg_o[name] = dout(name, shape)

    def lg_ap(d1, off=0):
        return bass.AP(d1.tensor, off, [[1, 128], [128, 16]])

    with ExitStack() as ST:
        def sb(name, shape, dt=F32, st=None):
            return (st or ST).enter_context(nc.sbuf_tensor(name, list(shape), dt))

        banks = []
        for i in range(8):
            pt = ST.enter_context(nc.psum_tensor("bank%d" % i, [128, 512], F32))
            banks.append(T(pt, "bank%d" % i))
        bank_rr = [0]

        def nbank():
            b = banks[bank_rr[0]]
            bank_rr[0] = (bank_rr[0] + 1) % 8
            return b

        identb = sb("identb", [128, 128], BF16); Tidb = T(identb)
        identf = sb("identf", [128, 128]); Tidf = T(identf)
        gain_bc = sb("gain_bc", [128, 1024]); Tgain = T(gain_bc)

        def norm_front(tt, xt, Txt, xn, Txn, ss, Tss, junk, Tjunk, dma=True, part=3):
            if dma:
                P.dma(xt[:], xall[tt * 128:(tt + 1) * 128, :], [], [Txt])
            if part & 1:
                P.act(lambda e: e.activation(junk[:], xt[:], AF.Square, scale=1.0 / 32, accum_out=ss[:, 0:1]), [Txt], [Tjunk, Tss])
                P.act(lambda e: e.activation(ss[:, 3:4], ss[:, 0:1], AF.Ln, bias=EPS), [Tss], [Tss])
                P.act(lambda e: e.activation(ss[:, 2:3], ss[:, 3:4], AF.Exp, scale=-0.5), [Tss], [Tss])
            if not (part & 2):
                return
            P.dve(lambda e: e.scalar_tensor_tensor(xn[:], xt[:], ss[:, 2:3], gain_bc[:], ALU.mult, ALU.mult),
                  [Txt, Tss, Tgain], [Txn])

        def norm_back(tt, xn, Txn, xnT, TxnT, col0, xnTp=None, TxnTp=None):
            bk = nbank()
            pb = bk.ap[:].bitcast(BF16)

            def tr(e):
                r = None
                for k in range(8):
                    r = e.transpose(pb[:, k * 128:(k + 1) * 128], xn[:, k * 128:(k + 1) * 128], identb[:])
                return r
            P.pe(tr, [Txn, Tidb], [bk])
            P.act(lambda e: e.activation(xnT[:, :, col0:col0 + 128],
                                         pb[:, 0:1024].rearrange("p (k t) -> p k t", k=8), AF.Copy),
                  [bk], [TxnT])
            if xnTp is not None:
                P.act(lambda e: e.activation(xnTp[:].rearrange("p k (t c) -> p k t c", t=4),
                                             pb[:, 0:1024].rearrange("p (k c t) -> p k t c", k=8, t=4), AF.Copy),
                      [bk], [TxnTp])

        def load_norm_tile(tt, xt, Txt, xn, Txn, xnT, TxnT, col0, ss, Tss, junk, Tjunk, xnTp=None, TxnTp=None):
            norm_front(tt, xt, Txt, xn, Txn, ss, Tss, junk, Tjunk)
            norm_back(tt, xn, Txn, xnT, TxnT, col0, xnTp, TxnTp)

        attnT = sb("attnT", [128, 4, NTOK], BF16)
        TattnT = [[T(attnT, "attnT%d_%d" % (c, t)) for t in range(NT)] for c in range(4)]
        zT = sb("zT", [128, 4, NTOK], BF16)
        TzT = [T(zT, "zT%d" % t) for t in range(NT)]
        with ExitStack() as S1:
            def walloc(name, kch, ncols, st):
                t = sb(name, [128, kch, ncols], BF16, st)
                return t, T(t, name)
            wq, Twq = walloc("wq", 8, 512, S1)
            wkv, Twkv = walloc("wkv", 8, 256, S1)
            wu, Twu = walloc("wu", 8, 512, S1)


            pre = {}
            for _n, _sh, _dt in (("tb", [128, 40, 16], F32), ("Tbf", [128, 16, 128], BF16), ("PXr", [128, 16, 128], BF16),
                                 ("PXi", [128, 16, 128], BF16), ("QRb", [128, 16, 128], BF16), ("QIb", [128, 16, 128], BF16),
                                 ("COS", [128, 16, 32], F32), ("SIN", [128, 16, 32], F32), ("RHO", [128, 16, 32], F32),
                                 ("h0rs", [128, 16], F32),
                                 ("h0is", [128, 16], F32), ("pw", [128, 5, 8, 16], F32)):
                pre[_n] = sb(_n, _sh, _dt, S1)
            SSbox = [ExitStack()]

            def s1(name, shape, dt=F32):
                if name in pre:
                    return pre[name]
                return sb(name, shape, dt, SSbox[0] if SSbox[0] is not None else S1)

            P.dma(identf[:], ident_f[:], [], [Tidf])
            are = s1("are", [128, 16]); aim = s1("aim", [128, 16]); ldt = s1("ldt", [128, 16])
            Tare, Taim, Tldt = T(are), T(aim), T(ldt)
            P.dma(are[:], a_re[:], [], [Tare])
            P.dma(aim[:], a_im[:], [], [Taim], q="act")
            P.dma(ldt[:], log_dt[:], [], [Tldt])
            jt = s1("jt", [128, 8]); Tjt = T(jt)
            P.dma(jt[:], dram_bc_rows(jtab, 128), [], [Tjt], q="act")
            kix = s1("kix", [128, 64]); Tkix = T(kix)
            P.dma(kix[:], dram_bc_rows(kidx, 128), [], [Tkix], q="act")
            bre = s1("bre", [128, 16, 16]); bim = s1("bim", [128, 16, 16]); Tbre, Tbim = T(bre), T(bim)
            P.dma(bre[:].rearrange("p g c -> p (g c)"), b_re[:], [], [Tbre])
            P.dma(bim[:].rearrange("p g c -> p (g c)"), b_im[:], [], [Tbim], q="act")
            crT = s1("crT", [128, 16, 16]); ciT = s1("ciT", [128, 16, 16]); TcrT, TciT = T(crT), T(ciT)
            P.dma(crT[:].rearrange("p g c -> p (g c)"), c_re[:], [], [TcrT])
            P.dma(ciT[:].rearrange("p g c -> p (g c)"), c_im[:], [], [TciT], q="act")
            dcol = s1("dcol", [128, 16]); Tdcol = T(dcol)
            P.dma(dcol[:], ssm_d[:], [], [Tdcol])
            P.dma(identb[:], ident_b[:], [], [Tidb])
            P.dma(gain_bc[:], dram_bc_rows(norm_gain, 128), [], [Tgain])
            maskt = s1("maskt", [128, 128]); Tmaskt = T(maskt)
            P.dma(maskt[:], maskT[:], [], [Tmaskt])

            _param_tiles = [Tare, Taim, Tldt, Tbre, Tbim, TcrT, TciT, Tdcol]
            for (_t, _T, _src) in ((wq, Twq, w_in[:, 0:512]), (wkv, Twkv, w_in[:, 512:768]), (wu, Twu, w_in[:, 1280:1792])):
                P.dma(_t[:], _src.rearrange("(k p) n -> p k n", p=128), _param_tiles, [_T], q="pool")

            tb = s1("tb", [128, 40, 16]); Ttb = T(tb)
            _slot = [0]

            def slot():
                i = _slot[0]
                _slot[0] += 1
                assert i < 40
                return tb[:, i, :]

            def V(fn):
                P.dve(fn, [Ttb, Tare, Taim, Tldt], [Ttb])

            def A(fn):
                P.act(fn, [Ttb, Tare, Taim, Tldt], [Ttb])

            dt_ = slot(); A(lambda e: e.activation(dt_, ldt[:], AF.Exp))
            ard = slot(); V(lambda e: e.tensor_tensor(ard, are[:], dt_, ALU.mult))
            aid = slot(); V(lambda e: e.tensor_tensor(aid, aim[:], dt_, ALU.mult))

            def sincos(ang_fns, shape_like, sin_out, cos_out, tmp1, tmp2, V_, A_):
                for (dst, shift) in ((sin_out, 0.0), (cos_out, PI / 2)):
                    for f in ang_fns(shift):
                        V_(f)
                    V_(lambda e: e.tensor_scalar(tmp2, tmp1, 1.0 / (2 * PI), 12582912.0, ALU.mult, ALU.add))
                    V_(lambda e: e.tensor_scalar(tmp2, tmp2, -12582912.0, None, ALU.add))
                    V_(lambda e: e.scalar_tensor_tensor(tmp1, tmp2, -2 * PI, tmp1, ALU.mult, ALU.add))
                    V_(lambda e: e.tensor_scalar(tmp1, tmp1, PI, -PI, ALU.min, ALU.max))
                    A_(lambda e, dst=dst: e.activation(dst, tmp1, AF.Sin))

            JL = (-1, -2, -3, -4, 1, 2, 3, 4)
            pw = s1("pw", [128, 5, 8, 16]); Tpw = T(pw)
            jb = bc_last(jt[:].rearrange("p (j o) -> p j o", o=1), 16)
            ardb = bass.AP(tb, ard.offset, [list(ard.ap[0]), [0, 8], [1, 16]])
            aidb = bass.AP(tb, aid.offset, [list(aid.ap[0]), [0, 8], [1, 16]])
            t1, t2 = slot(), slot()

            def Vp(fn):
                P.dve(fn, [Ttb, Tjt, Tpw], [Tpw])

            def Ap(fn):
                P.act(fn, [Tpw], [Tpw])
            Vp(lambda e: e.tensor_tensor(pw[:, 2], jb, ardb, ALU.mult))
            Ap(lambda e: e.activation(pw[:, 2], pw[:, 2], AF.Exp))

            def angj(sh):
                return [lambda e: e.tensor_tensor(pw[:, 0], jb, aidb, ALU.mult),
                        lambda e, sh=sh: e.tensor_scalar(pw[:, 0], pw[:, 0], sh, None, ALU.add)]
            sincos(angj, None, pw[:, 3], pw[:, 4], pw[:, 0], pw[:, 1], Vp, Ap)
            Vp(lambda e: e.tensor_tensor(pw[:, 3], pw[:, 3], pw[:, 2], ALU.mult))
            Vp(lambda e: e.tensor_tensor(pw[:, 4], pw[:, 4], pw[:, 2], ALU.mult))
            pr, pim = {}, {}
            for ji, j in enumerate(JL):
                pr[j], pim[j] = pw[:, 4, ji, :], pw[:, 3, ji, :]
            V(lambda e: e.tensor_copy(t1, pw[:, 4, 7, :]))
            P.dve(lambda e: e.tensor_copy(t2, pw[:, 3, 7, :]), [Tpw, Ttb], [Ttb])
            den = slot(); wr = slot(); wi = slot(); am1 = slot()
            V(lambda e: e.tensor_tensor(den, are[:], are[:], ALU.mult))
            V(lambda e: e.tensor_tensor(t1, aim[:], aim[:], ALU.mult))
            V(lambda e: e.tensor_tensor(den, den, t1, ALU.add))
            V(lambda e: e.reciprocal(den, den))
            V(lambda e: e.tensor_scalar(am1, pr[1], -1.0, None, ALU.add))
            V(lambda e: e.tensor_tensor(wr, am1, are[:], ALU.mult))
            V(lambda e: e.tensor_tensor(t1, pim[1], aim[:], ALU.mult))
            V(lambda e: e.tensor_tensor(wr, wr, t1, ALU.add))
            V(lambda e: e.tensor_tensor(wr, wr, den, ALU.mult))
            V(lambda e: e.tensor_tensor(wi, pim[1], are[:], ALU.mult))
            V(lambda e: e.tensor_tensor(t1, am1, aim[:], ALU.mult))
            V(lambda e: e.tensor_tensor(wi, wi, t1, ALU.subtract))
            V(lambda e: e.tensor_tensor(wi, wi, den, ALU.mult))

            tmpA = s1("tmpA", [128, 16, 16]); tmpB = s1("tmpB", [128, 16, 16]); Ttmp = T(tmpA)

            def cmul(dst_r, dst_i, Tdst, sr, si, xr, xi, Tx):
                srb = bc_last(sr.rearrange("p (g o) -> p g o", o=1), 16)
                sib = bc_last(si.rearrange("p (g o) -> p g o", o=1), 16)
                rd = [Ttb, Ttmp] + Tx
                P.dve(lambda e: e.tensor_tensor(tmpA[:], xr, srb, ALU.mult), rd, [Ttmp])
                P.dve(lambda e: e.tensor_tensor(tmpB[:], xi, sib, ALU.mult), rd, [Ttmp])
                P.dve(lambda e: e.tensor_tensor(dst_r, tmpA[:], tmpB[:], ALU.subtract), [Ttmp], Tdst)
                P.dve(lambda e: e.tensor_tensor(tmpA[:], xi, srb, ALU.mult), rd, [Ttmp])
                P.dve(lambda e: e.tensor_tensor(tmpB[:], xr, sib, ALU.mult), rd, [Ttmp])
                P.dve(lambda e: e.tensor_tensor(dst_i, tmpA[:], tmpB[:], ALU.add), [Ttmp], Tdst)

            bbr = s1("bbr", [128, 16, 16]); bbi = s1("bbi", [128, 16, 16]); Tbb = T(bbr)
            cmul(bbr[:], bbi[:], [Tbb], wr, wi, bre[:], bim[:], [Tbre, Tbim])

            PPr = s1("PPr", [128, 16, 128]); PPi = s1("PPi", [128, 16, 128])
            PTr = s1("PTr", [128, 16, 128]); PTi = s1("PTi", [128, 16, 128])
            QRf = s1("QRf", [128, 16, 128]); QIf = s1("QIf", [128, 16, 128])
            TPP, TPT, TQ = T(PPr), T(PTr), T(QRf)
            for m, Tm in ((PPr, TPP), (PPi, TPP), (PTr, TPT), (PTi, TPT), (QRf, TQ), (QIf, TQ)):
                P.pool(lambda e, m=m: e.memset(m[:], 0.0), [], [Tm])
            c4r = s1("c4r", [128, 16, 4, 16]); c4i = s1("c4i", [128, 16, 4, 16]); Tc4 = T(c4r)
            p4r = s1("p4r", [128, 16, 4, 16]); p4i = s1("p4i", [128, 16, 4, 16]); Tp4 = T(p4r)
            u4a = s1("u4a", [128, 16, 4, 16]); u4b = s1("u4b", [128, 16, 4, 16]); Tu4 = T(u4a)
            pst = list(pw[:].ap[0])

            def pw4(comp, j0):
                off = pw[:].offset + comp * 128 + j0 * 16
                return bass.AP(pw, off, [pst, [1, 16], [16, 4], [0, 16]])

            def bc4(t3):
                a_ = t3[:]
                return bass.AP(t3, a_.offset, [list(a_.ap[0]), [16, 16], [0, 4], [1, 16]])

            def sc4(sl_):
                return bass.AP(sl_.tensor, sl_.offset, [list(sl_.ap[0]), [1, 16], [0, 4], [0, 16]])

            def cmul4(dr, di, Tdst, sr, si, xr, xi, rd):
                rd = rd + [Tu4, Tpw, Ttb]
                P.dve(lambda e: e.tensor_tensor(u4a[:], xr, sr, ALU.mult), rd, [Tu4])
                P.dve(lambda e: e.tensor_tensor(u4b[:], xi, si, ALU.mult), rd, [Tu4])
                P.dve(lambda e: e.tensor_tensor(dr[:], u4a[:], u4b[:], ALU.subtract), [Tu4], [Tdst])
                P.dve(lambda e: e.tensor_tensor(u4a[:], xi, sr, ALU.mult), rd, [Tu4])
                P.dve(lambda e: e.tensor_tensor(u4b[:], xr, si, ALU.mult), rd, [Tu4])
                P.dve(lambda e: e.tensor_tensor(di[:], u4a[:], u4b[:], ALU.add), [Tu4], [Tdst])

            def place4(dst, Tdst, src, Tsrc, scale=1.0):
                dv = dst[:].rearrange("p g (x h c) -> p g x h c", x=4, h=2)
                for h in range(2):
                    P.dve(lambda e, h=h: e.tensor_scalar(dv[h * 64:(h + 1) * 64, :, :, h, :], src[h * 64:(h + 1) * 64], scale, None, ALU.mult),
                          [Tsrc], [Tdst])
            cmul4(p4r, p4i, Tp4, pw4(4, 0), pw4(3, 0), bc4(bbr), bc4(bbi), [Tbb])
            place4(PTr, TPT, p4r, Tp4)
            place4(PTi, TPT, p4i, Tp4)
            cmul4(c4r, c4i, Tc4, sc4(pr[4]), sc4(pim[4]), p4r[:], p4i[:], [Tp4])
            place4(PPr, TPP, c4r, Tc4)
            place4(PPi, TPP, c4i, Tc4)
            cmul4(c4r, c4i, Tc4, pw4(4, 4), pw4(3, 4), bc4(crT), bc4(ciT), [TcrT, TciT])
            place4(QRf, TQ, c4r, Tc4)
            place4(QIf, TQ, c4i, Tc4, scale=-1.0)

            Tbf = s1("Tbf", [128, 16, 128], BF16); PXr = s1("PXr", [128, 16, 128], BF16)
            PXi = s1("PXi", [128, 16, 128], BF16); QRb = s1("QRb", [128, 16, 128], BF16)
            QIb = s1("QIb", [128, 16, 128], BF16)
            TTbf, TPX, TQb = T(Tbf), T(PXr), T(QRb)
            P.act(lambda e: e.activation(QRb[:], QRf[:], AF.Copy), [TQ], [TQb])
            P.act(lambda e: e.activation(QIb[:], QIf[:], AF.Copy), [TQ], [TQb])
            tmk = s1("tmk", [128, 128]); Ttmk = T(tmk)
            for gp in range(16):
                bk = nbank()

                def mmT(e, gp=gp, bk=bk):
                    e.matmul(bk.ap[:, 0:128], PTr[:, gp, :], QRf[:, gp, :], start=True, stop=False)
                    e.matmul(bk.ap[:, 0:128], PTi[:, gp, :], QIf[:, gp, :], start=False, stop=True)
                    e.transpose(bk.ap[:, 128:256], PPr[:, gp, :], identf[:])
                    return e.transpose(bk.ap[:, 256:384], PPi[:, gp, :], identf[:])
                P.pe(mmT, [TPT, TQ, TPP, Tidf], [bk])
                P.dve(lambda e, bk=bk: e.tensor_tensor(tmk[:], bk.ap[:, 0:128], maskt[:], ALU.mult), [bk, Tmaskt], [Ttmk])
                P.dve(lambda e, gp=gp: e.scalar_tensor_tensor(Tbf[:, gp, :], identf[:], dcol[:, gp:gp + 1], tmk[:], ALU.mult, ALU.add),
                      [Ttmk, Tidf, Tdcol], [TTbf])
                P.act(lambda e, gp=gp, bk=bk: e.activation(PXr[:, gp, :], bk.ap[:, 128:256], AF.Copy), [bk], [TPX])
                P.act(lambda e, gp=gp, bk=bk: e.activation(PXi[:, gp, :], bk.ap[:, 256:384], AF.Copy), [bk], [TPX])

            COS = s1("COS", [128, 16, 32]); SIN = s1("SIN", [128, 16, 32]); RHO = s1("RHO", [128, 16, 32])
            tg1 = s1("tg1", [128, 16, 32]); tg2 = s1("tg2", [128, 16, 32])
            TCOS, Ttg = T(COS), T(tg1)
            kb = bass.AP(kix, kix[:].offset, [list(kix[:].ap[0]), [0, 16], [1, 32]])
            th4 = slot()
            V(lambda e: e.tensor_scalar(th4, aid, 4.0, None, ALU.mult))
            th4b = bc_last(th4.rearrange("p (g o) -> p g o", o=1), 32)
            ard4b = bc_last(ard.rearrange("p (g o) -> p g o", o=1), 32)

            def Vt(fn):
                P.dve(fn, [Ttb, Tkix, Ttg, TCOS], [Ttg, TCOS])

            def At(fn):
                P.act(fn, [Ttb, Tkix, Ttg, TCOS], [Ttg, TCOS])

            def angk(sh):
                return [lambda e: e.tensor_tensor(tg1[:], kb, th4b, ALU.mult),
                        lambda e, sh=sh: e.tensor_scalar(tg1[:], tg1[:], sh, None, ALU.add)]
            sincos(angk, None, SIN[:], COS[:], tg1[:], tg2[:], Vt, At)
            Vt(lambda e: e.tensor_scalar(tg1[:], kb, 0.0, None, ALU.mult))
            Vt(lambda e: e.tensor_tensor(tg1[:], tg1[:], ard4b, ALU.add))
            At(lambda e: e.activation(RHO[:], tg1[:], AF.Exp, scale=4.0))
            Vt(lambda e: e.memset(RHO[:, :, 0:1], 0.0))

            h0rs = s1("h0rs", [128, 16]); h0is = s1("h0is", [128, 16]); Th0 = T(h0rs)
            P.dma(h0rs[:], h0r[:], [], [Th0])
            P.dma(h0is[:], h0i[:], [], [Th0], q="act")

            if "ssm_mats" in dbg_o:
                dm = s1("dm", [128, 4, 128]); Tdm = T(dm)
                P.act(lambda e: e.activation(dm[:, 0, :], Tbf[:, 3, :], AF.Copy), [TTbf], [Tdm])
                P.act(lambda e: e.activation(dm[:, 1, :], PXr[:, 3, :], AF.Copy), [TPX], [Tdm])
                P.act(lambda e: e.activation(dm[:, 2, :], QRb[:, 3, :], AF.Copy), [TQb], [Tdm])
                P.act(lambda e: e.activation(dm[:, 3, :], QIb[:, 3, :], AF.Copy), [TQb], [Tdm])
                P.dma(dbg_o["ssm_mats"].rearrange("(a p) n -> p a n", p=128), dm[:], [Tdm], [], is_out=True)

            P.barrier()
            if stop_after == 0:
                P.emit()
                SSbox[0].close()
                return nc
            SSbox[0].close()
            SSbox[0] = None
            es = s1("es", [64, 8]); Tes = T(es)
            P.dma(es[:], dram_bc_rows(sinks, 64), [], [Tes])
            P.act(lambda e: e.activation(es[:], es[:], AF.Exp), [Tes], [Tes])
            esk = s1("esk", [64, 2, 512]); Tesk = T(esk)
            for g in range(2):
                for hf in range(2):
                    for c in range(2):
                        h = 2 * (2 * g + c) + hf
                        col = (hf * 2 + c) * 128
                        P.dve(lambda e, g=g, h=h, col=col: e.tensor_copy(esk[:, g, col:col + 128], bc_last(es[:, h:h + 1], 128)),
                              [Tes], [Tesk])
            esrow = s1("esrow", [1, 2, 512], BF16); Tesrow = T(esrow)
            P.dve(lambda e: e.tensor_copy(esrow[:], esk[0:1, :, :]), [Tesk], [Tesrow])
            ones_b = s1("ones_b", [128, 64], BF16); Tones = T(ones_b)
            P.pool(lambda e: e.memset(ones_b[:], 1.0), [], [Tones])

            NSL = 7
            KP = [s1("KP%d" % i, [128, NSL, 128], BF16) for i in range(4)]
            Vb = s1("Vb", [128, NSL, 128], BF16)
            TkT = [T(KP[0], "kT%d" % i) for i in range(NSL)]
            TkS = [T(KP[1], "kS%d" % i) for i in range(NSL)]
            for i in range(4):
                P.pool(lambda e, i=i: e.memset(KP[i][:], 0.0), [], TkT + TkS)
            TVb = [T(Vb, "V%d" % i) for i in range(NSL)]

            def kslot(j, tt):
                return 6 if (tt == 16 and j == 15) else j % 6

            xt3 = [s1("xt%d" % i, [128, 1024]) for i in range(3)]; Txt3 = [T(t) for t in xt3]
            xn2 = [s1("xn%d" % i, [128, 1024], BF16) for i in range(2)]; Txn2 = [T(t) for t in xn2]
            junk = s1("junk", [128, 1024], BF16); Tjunk = T(junk)
            ss2 = [s1("ss%d" % i, [128, 4]) for i in range(2)]; Tss2 = [T(t) for t in ss2]
            xnT2 = [s1("xnT%d" % i, [128, 8, 128], BF16) for i in range(2)]; TxnT2 = [T(t) for t in xnT2]
            xnTp2 = [s1("xnTp%d" % i, [128, 8, 128], BF16) for i in range(2)]; TxnTp2 = [T(t) for t in xnTp2]
            qT2 = [s1("qT%d" % i, [128, 4, 128], BF16) for i in range(2)]; TqT2 = [T(t) for t in qT2]
            rt3 = [s1("rt%d" % i, [128, 16]) for i in range(3)]; Trt3 = [T(t) for t in rt3]
            qf2 = [s1("qf%d" % i, [128, 512]) for i in range(2)]; Tqf2 = [T(t) for t in qf2]
            kvf2 = [s1("kvf%d" % i, [128, 256]) for i in range(2)]; Tkvf2 = [T(t) for t in kvf2]
            qb2 = [s1("qb%d" % i, [128, 512], BF16) for i in range(2)]; Tqb2 = [T(t) for t in qb2]
            kb2 = [s1("kb%d" % i, [128, 128], BF16) for i in range(2)]; Tkb2 = [T(t) for t in kb2]
            rr = [s1("rr%d" % i, [128, 8, 8]) for i in range(4)]; Trr = T(rr[0])
            utok = s1("utok", [128, 512], BF16); Tutok = T(utok)
            Ub4 = [s1("Ub%d" % i, [128, 16, 32], BF16) for i in range(4)]; TUb4 = [T(t) for t in Ub4]
            pT2 = [s1("pT%d" % i, [128, 4, 512], BF16) for i in range(2)]; TpT2 = [T(t) for t in pT2]
            at1g = [s1("at1_%d" % i, [64, 512]) for i in range(2)]; at2g = [s1("at2_%d" % i, [64, 512]) for i in range(2)]
            Tatg = [T(t) for t in at1g]
            ckf = s1("ckf", [128, 128]); Tckf = T(ckf)
            ckb = s1("ckb", [128, 128], BF16); Tckb = T(ckb)
            zr = s1("zr", [128, 16, 32]); zi = s1("zi", [128, 16, 32]); za = s1("za", [128, 16, 32]); zb = s1("zb", [128, 16, 32])
            Tz = T(zr)
            gr = s1("gr", [128, 16, 32]); gi = s1("gi", [128, 16, 32]); Tg = T(gr)
            HXr2 = [s1("HXr%d" % i, [128, 16, 33], BF16) for i in range(2)]
            HXi2 = [s1("HXi%d" % i, [128, 16, 33], BF16) for i in range(2)]
            THX2 = [T(t) for t in HXr2]
            hz = s1("hz", [128, 16]); Thz = T(hz)
            P.pool(lambda e: e.memset(hz[:], 0.0), [], [Thz])
            hsr = s1("hsr", [128, 16]); hsi = s1("hsi", [128, 16]); Ths = T(hsr)
            ysb = s1("ysb", [128, 512], BF16); Tysb = T(ysb)
            hcr = s1("hcr", [128, 16]); hci = s1("hci", [128, 16]); Thc = T(hcr)
            ztk = s1("ztk", [128, 512], BF16); Tztk = T(ztk)
            C32, S32 = COS[:], SIN[:]
            a4r = pr[4].rearrange("p (g o) -> p g o", o=1)
            a4i = pim[4].rearrange("p (g o) -> p g o", o=1)
            fl = lambda t: t[:].rearrange("p g k -> p (g k)")

            def I_Ab(tt):
                bi = tt % 2
                norm_back(tt, xn2[bi], Txn2[bi], xnT2[bi], TxnT2[bi], 0, xnTp=xnTp2[bi], TxnTp=TxnTp2[bi])

            def I_A0(tt):
                b3 = tt % 3
                P.dma(xt3[b3][:], xall[tt * 128:(tt + 1) * 128, :], [], [Txt3[b3]])
                P.dma(rt3[b3][:], ropetok[tt * 128:(tt + 1) * 128, :], [], [Trt3[b3]])

            def I_As(tt):
                bi = tt % 2
                norm_front(tt, xt3[tt % 3], Txt3[tt % 3], xn2[bi], Txn2[bi], ss2[bi], Tss2[bi], junk, Tjunk, dma=False, part=1)

            def I_A(tt):
                bi = tt % 2
                norm_front(tt, xt3[tt % 3], Txt3[tt % 3], xn2[bi], Txn2[bi], ss2[bi], Tss2[bi], junk, Tjunk, dma=False, part=2)

            def I_B(tt):
                bi = tt % 2
                xnT, TxnT = xnT2[bi], TxnT2[bi]
                xnTp, TxnTp = xnTp2[bi], TxnTp2[bi]
                qT, TqT = qT2[bi], TqT2[bi]
                sl = kslot(tt, tt)
                bk = nbank()
                Ub, TUb = Ub4[tt % 4], TUb4[tt % 4]

                def mmu(e, bk=bk, xnTp=xnTp):
                    r = None
                    for k in range(8):
                        r = e.matmul(bk.ap[:, :], xnTp[:, k, :], wu[:, k, :], start=(k == 0), stop=(k == 7))
                    return r
                P.pe(mmu, [Twu, TxnTp], [bk])
                P.act(lambda e, bk=bk: e.activation(utok[:], bk.ap[:, :], AF.Copy), [bk], [Tutok])
                P.dve(lambda e, Ub=Ub: e.transpose(Ub[:].rearrange("p g k -> p (g k)"), utok[:]), [Tutok], [TUb])
                bq = nbank()
                bkv = nbank()

                def mmqkv(e, bq=bq, bkv=bkv, xnT=xnT):
                    r = None
                    for k in range(8):
                        e.matmul(bq.ap[:, :], xnT[:, k, :], wq[:, k, :], start=(k == 0), stop=(k == 7))
                    for k in range(8):
                        r = e.matmul(bkv.ap[:, 0:256], xnT[:, k, :], wkv[:, k, :], start=(k == 0), stop=(k == 7))
                    return r
                P.pe(mmqkv, [Twq, Twkv, TxnT], [bq, bkv])
                qf, Tqf = qf2[bi], Tqf2[bi]
                kvf, Tkvf = kvf2[bi], Tkvf2[bi]
                P.act(lambda e, bq=bq, qf=qf: e.activation(qf[:], bq.ap[:, :], AF.Copy), [bq], [Tqf])
                P.act(lambda e, bkv=bkv, kvf=kvf: e.activation(kvf[:], bkv.ap[:, 0:256], AF.Copy), [bkv], [Tkvf])
                rp, Trp = rt3[tt % 3], Trt3[tt % 3]
                for (src, Tsrc, nh, dst_b, Tdst_b, tag) in ((qf, Tqf, 8, qb2[bi], Tqb2[bi], "q"), (kvf, Tkvf, 2, kb2[bi], Tkb2[bi], "k")):
                    xv = src[:, 0:nh * 64].rearrange("p (h d) -> p h d", d=64)
                    cb = bass.AP(rp, rp[:].offset, [list(rp[:].ap[0]), [0, nh], [1, 8]])
                    sb_ = bass.AP(rp, rp[:].offset + 8, [list(rp[:].ap[0]), [0, nh], [1, 8]])
                    r1, r2, r3, r4 = (rr[i][:, 0:nh, :] for i in range(4))
                    P.pool(lambda e, xv=xv, cb=cb, r1=r1: e.tensor_tensor(r1, xv[:, :, 0:8], cb, ALU.mult), [Tsrc, Trp, Trr], [Trr])
                    P.pool(lambda e, xv=xv, sb_=sb_, r2=r2: e.tensor_tensor(r2, xv[:, :, 8:16], sb_, ALU.mult), [Tsrc, Trp, Trr], [Trr])
                    P.pool(lambda e, xv=xv, cb=cb, r3=r3: e.tensor_tensor(r3, xv[:, :, 8:16], cb, ALU.mult), [Tsrc, Trp, Trr], [Trr])
                    P.pool(lambda e, xv=xv, sb_=sb_, r4=r4: e.tensor_tensor(r4, xv[:, :, 0:8], sb_, ALU.mult), [Tsrc, Trp, Trr], [Trr])
                    P.pool(lambda e, xv=xv, r1=r1, r2=r2: e.tensor_tensor(xv[:, :, 0:8], r1, r2, ALU.subtract), [Trr], [Tsrc])
                    P.pool(lambda e, xv=xv, r3=r3, r4=r4: e.tensor_tensor(xv[:, :, 8:16], r3, r4, ALU.add), [Trr], [Tsrc])
                    P.pool(lambda e, src=src, dst_b=dst_b, nh=nh: e.tensor_copy(dst_b[:], src[:, 0:nh * 64]), [Tsrc], [Tdst_b])
                P.pool(lambda e, kvf=kvf, sl=sl: e.tensor_copy(Vb[:, sl, :], kvf[:, 128:256]), [Tkvf], [TVb[sl]])
                if tt == 15:
                    P.dma(kp_o[:], kvf[:, 0:128], [Tkvf], [], is_out=True)
                    P.dma(vp_o[:], kvf[:, 128:256], [Tkvf], [], is_out=True)
                elif tt == 16:
                    P.dma(ks_o[64:128, :], kvf[0:64, 0:128], [Tkvf], [], is_out=True)
                    P.dma(ks_o[0:64, :], ck[64:128, :], [], [], is_out=True)
                    P.dma(vs_o[64:128, :], kvf[0:64, 128:256], [Tkvf], [], is_out=True)
                    P.dma(vs_o[0:64, :], cv[64:128, :], [], [], is_out=True)

            def I_B2(tt):
                bi = tt % 2
                qT, TqT = qT2[bi], TqT2[bi]
                sl = kslot(tt, tt)
                bt = nbank()
                pbt = bt.ap[:].bitcast(BF16)
                qb, kb = qb2[bi], kb2[bi]

                def trqk(e, pbt=pbt, qb=qb, kb=kb):
                    for c in range(4):
                        e.transpose(pbt[:, c * 128:(c + 1) * 128], qb[:, c * 128:(c + 1) * 128], identb[:])
                    return e.transpose(pbt[:, 512:640], kb[:], identb[:])
                P.pe(trqk, [Tqb2[bi], Tkb2[bi], Tidb], [bt])
                P.act(lambda e, pbt=pbt, qT=qT: e.activation(qT[:], pbt[:, 0:512].rearrange("p (c t) -> p c t", c=4), AF.Copy), [bt], [TqT])
                P.act(lambda e, pbt=pbt, sl=sl: e.activation(KP[0][0:64, sl, :], pbt[0:64, 512:640], AF.Copy), [bt], [TkT[sl]])
                P.act(lambda e, pbt=pbt, sl=sl: e.activation(KP[3][64:128, sl, :], pbt[64:128, 512:640], AF.Copy), [bt], [TkT[sl]])
                P.pool(lambda e, sl=sl: e.tensor_copy(KP[1][64:128, sl, :], KP[0][0:64, sl, :]), [TkT[sl]], [TkS[sl]])
                P.pool(lambda e, sl=sl: e.tensor_copy(KP[2][0:64, sl, :], KP[3][64:128, sl, :]), [TkT[sl]], [TkS[sl]])
                if tt == 16:
                    psl = 6
                    P.dma(ckf[:], ck[:], [], [Tckf])
                    P.act(lambda e: e.activation(ckb[:], ckf[:], AF.Copy), [Tckf], [Tckb])
                    bk3 = nbank()
                    pb3 = bk3.ap[:].bitcast(BF16)
                    P.pe(lambda e: e.transpose(pb3[:, 0:128], ckb[:], identb[:]), [Tckb, Tidb], [bk3])
                    P.act(lambda e: e.activation(KP[0][0:64, psl, :], pb3[0:64, 0:128], AF.Copy), [bk3], [TkT[psl]])
                    P.act(lambda e: e.activation(KP[3][64:128, psl, :], pb3[64:128, 0:128], AF.Copy), [bk3], [TkT[psl]])
                    P.pool(lambda e: e.tensor_copy(KP[1][64:128, psl, :], KP[0][0:64, psl, :]), [TkT[psl]], [TkS[psl]])
                    P.pool(lambda e: e.tensor_copy(KP[2][0:64, psl, :], KP[3][64:128, psl, :]), [TkT[psl]], [TkS[psl]])
                    P.dma(Vb[:, psl, :], cv[:], [], [TVb[psl]], q="pool")

            def I_C(tt):
                bi = tt % 2
                qT, TqT = qT2[bi], TqT2[bi]
                pT, TpT = pT2[bi], TpT2[bi]
                Ub, TUb = Ub4[tt % 4], TUb4[tt % 4]
                jl = [tt] if tt == 0 else [tt - 1, tt]
                for g in range(2):
                    for ji, j in enumerate(jl):
                        jsl = kslot(j, tt)
                        bk = nbank()

                        def mms(e, bk=bk, g=g, jsl=jsl, qT=qT):
                            r = None
                            for hf in range(2):
                                r = e.matmul(bk.ap[:, hf * 256:(hf + 1) * 256], KP[2 * g + hf][:, jsl, :],
                                             qT[:, 2 * g:2 * g + 2, :], start=True, stop=True)
                            return r
                        P.pe(mms, [TkT[jsl], TkS[jsl], TqT], [bk])
                        slot_p = ji if len(jl) == 2 else 1
                        pg = pT[:, slot_p, :] if g == 0 else None
                        P.act(lambda e, bk=bk, slot_p=slot_p, pT=pT, g=g: e.activation(pT[:, 2 * g + slot_p, :], bk.ap[:, :], AF.Exp, scale=0.125),
                              [bk], [TpT])
                        pv = pT[:, 2 * g + slot_p, :].rearrange("p (h q) -> p h q", q=128)
                        if j == tt:
                            P.pool(lambda e, pv=pv: e.memset(pv[64:128, :, 0:64], 0.0), [TpT], [TpT])
                        else:
                            P.pool(lambda e, pv=pv: e.memset(pv[0:64, :, 64:128], 0.0), [TpT], [TpT])

            def I_Cs(tt):
                bi = tt % 2
                Ub, TUb = Ub4[tt % 4], TUb4[tt % 4]
                HXr, HXi, THX = HXr2[bi], HXi2[bi], THX2[bi]
                if tt == 16:
                    csr, csi, Tcs_ = h0rs, h0is, Th0
                elif tt > 0:
                    csr, csi, Tcs_ = hcr, hci, Thc
                else:
                    csr, csi, Tcs_ = hz, hz, Thz
                c3r = csr[:].rearrange("p (g o) -> p g o", o=1)
                c3i = csi[:].rearrange("p (g o) -> p g o", o=1)
                P.dve(lambda e, HXr=HXr, c3r=c3r: e.tensor_copy(HXr[:, :, 0:1], c3r), [Tcs_], [THX])
                P.dve(lambda e, HXi=HXi, c3i=c3i: e.tensor_copy(HXi[:, :, 0:1], c3i), [Tcs_], [THX])
                bxr = nbank()
                bxi = nbank()

                def mmx(e, bxr=bxr, bxi=bxi, Ub=Ub):
                    r = None
                    for gp in range(16):
                        e.matmul(bxr.ap[:, gp * 32:(gp + 1) * 32], PXr[:, gp, :], Ub[:, gp, :], start=True, stop=True)
                        r = e.matmul(bxi.ap[:, gp * 32:(gp + 1) * 32], PXi[:, gp, :], Ub[:, gp, :], start=True, stop=True)
                    return r
                P.pe(mmx, [TPX, TUb], [bxr, bxi])
                xr_v = bxr.ap[:, :].rearrange("p (g k) -> p g k", k=32)
                xi_v = bxi.ap[:, :].rearrange("p (g k) -> p g k", k=32)
                rdz = [bxr, bxi, TCOS, Tz]
                P.dve(lambda e, bxr=bxr: e.tensor_tensor(fl(za), bxr.ap[:, :], fl(COS), ALU.mult), rdz, [Tz])
                P.dve(lambda e, bxi=bxi: e.tensor_tensor(fl(zb), bxi.ap[:, :], fl(SIN), ALU.mult), rdz, [Tz])
                P.dve(lambda e: e.tensor_tensor(fl(zr), fl(za), fl(zb), ALU.add), [Tz], [Tz])
                P.dve(lambda e, bxi=bxi: e.tensor_tensor(fl(za), bxi.ap[:, :], fl(COS), ALU.mult), rdz, [Tz])
                P.dve(lambda e, bxr=bxr: e.tensor_tensor(fl(zb), bxr.ap[:, :], fl(SIN), ALU.mult), rdz, [Tz])
                P.dve(lambda e: e.tensor_tensor(fl(zi), fl(za), fl(zb), ALU.subtract), [Tz], [Tz])
                rdh = [Tcs_, Ttb, Tz]
                P.dve(lambda e, c3r=c3r: e.tensor_tensor(za[:, :, 0:1], c3r, a4r, ALU.mult), rdh, [Tz])
                P.dve(lambda e, c3i=c3i: e.tensor_tensor(zb[:, :, 0:1], c3i, a4i, ALU.mult), rdh, [Tz])
                P.dve(lambda e: e.tensor_tensor(zr[:, :, 0:1], zr[:, :, 0:1], za[:, :, 0:1], ALU.add), [Tz], [Tz])
                P.dve(lambda e: e.tensor_tensor(zr[:, :, 0:1], zr[:, :, 0:1], zb[:, :, 0:1], ALU.subtract), [Tz], [Tz])
                P.dve(lambda e, c3i=c3i: e.tensor_tensor(za[:, :, 0:1], c3i, a4r, ALU.mult), rdh, [Tz])
                P.dve(lambda e, c3r=c3r: e.tensor_tensor(zb[:, :, 0:1], c3r, a4i, ALU.mult), rdh, [Tz])
                P.dve(lambda e: e.tensor_tensor(zi[:, :, 0:1], zi[:, :, 0:1], za[:, :, 0:1], ALU.add), [Tz], [Tz])
                P.dve(lambda e: e.tensor_tensor(zi[:, :, 0:1], zi[:, :, 0:1], zb[:, :, 0:1], ALU.add), [Tz], [Tz])
                P.dve(lambda e: e.tensor_tensor_scan(fl(gr), fl(RHO), fl(zr), 0.0, ALU.mult, ALU.add), [Tz, TCOS], [Tg])
                P.dve(lambda e: e.tensor_tensor_scan(fl(gi), fl(RHO), fl(zi), 0.0, ALU.mult, ALU.add), [Tz, TCOS], [Tg])
                hcr3 = hcr[:].rearrange("p (g o) -> p g o", o=1)
                hci3 = hci[:].rearrange("p (g o) -> p g o", o=1)
                P.dve(lambda e: e.tensor_tensor(fl(za), fl(gr), fl(COS), ALU.mult), [Tg, TCOS, Tz], [Tz])
                P.dve(lambda e: e.tensor_tensor(fl(zb), fl(gi), fl(SIN), ALU.mult), [Tg, TCOS, Tz], [Tz])
                P.dve(lambda e, HXr=HXr: e.tensor_tensor(HXr[:, :, 1:33], za[:], zb[:], ALU.subtract), [Tz], [THX])
                P.dve(lambda e: e.tensor_tensor(hcr3, za[:, :, 31:32], zb[:, :, 31:32], ALU.subtract), [Tz, Tcs_], [Thc])
                if tt == 16:
                    P.dve(lambda e: e.tensor_tensor(hsr[:].rearrange("p (g o) -> p g o", o=1), za[:, :, 15:16], zb[:, :, 15:16], ALU.subtract), [Tz], [Ths])
                P.dve(lambda e: e.tensor_tensor(fl(za), fl(gi), fl(COS), ALU.mult), [Tg, TCOS, Tz], [Tz])
                P.dve(lambda e: e.tensor_tensor(fl(zb), fl(gr), fl(SIN), ALU.mult), [Tg, TCOS, Tz], [Tz])
                P.dve(lambda e, HXi=HXi: e.tensor_tensor(HXi[:, :, 1:33], za[:], zb[:], ALU.add), [Tz], [THX])
                P.dve(lambda e: e.tensor_tensor(hci3, za[:, :, 31:32], zb[:, :, 31:32], ALU.add), [Tz, Tcs_], [Thc])
                if tt == 16:
                    P.dve(lambda e: e.tensor_tensor(hsi[:].rearrange("p (g o) -> p g o", o=1), za[:, :, 15:16], zb[:, :, 15:16], ALU.add), [Tz], [Ths])
                if tt == min(15, ntiles - 1):
                    P.dma(srp_o[:], hcr[:], [Thc], [], is_out=True)
                    P.dma(sip_o[:], hci[:], [Thc], [], is_out=True)
                if tt == 16:
                    P.dma(srs_o[:], hsr[:], [Ths], [], is_out=True)
                    P.dma(sis_o[:], hsi[:], [Ths], [], is_out=True)

            def I_D(tt):
                bi = tt % 2
                pT, TpT = pT2[bi], TpT2[bi]
                Ub, TUb = Ub4[tt % 4], TUb4[tt % 4]
                Hbr, Hbi, THb = HXr2[bi], HXi2[bi], THX2[bi]
                jl = [tt] if tt == 0 else [tt - 1, tt]
                for g in range(2):
                    bko = nbank()
                    bkd = nbank()

                    def mmo(e, bko=bko, bkd=bkd, g=g, jl=jl, pT=pT, tt=tt):
                        r = None
                        n = len(jl)
                        for ji, j in enumerate(jl):
                            sp_ = 2 * g + (ji if n == 2 else 1)
                            e.matmul(bko.ap[0:64, :], Vb[:, kslot(j, tt), g * 64:(g + 1) * 64], pT[:, sp_, :], start=(ji == 0), stop=(ji == n - 1))
                        for ji, j in enumerate(jl):
                            sp_ = 2 * g + (ji if n == 2 else 1)
                            e.matmul(bkd.ap[0:64, :], ones_b[:], pT[:, sp_, :], start=(ji == 0), stop=False)
                        return e.matmul(bkd.ap[0:64, :], ones_b[0:1, :], esrow[0:1, g, :], start=False, stop=True)
                    P.pe(mmo, [TVb[kslot(j, tt)] for j in jl] + [TpT, Tones, Tesrow], [bko, bkd])
                    at1, at2, Tat = at1g[g], at2g[g], Tatg[g]
                    P.act(lambda e, bkd=bkd, at2=at2: e.activation(at2[:], bkd.ap[0:64, :], AF.Ln), [bkd], [Tat])
                    P.act(lambda e, at2=at2: e.activation(at2[:], at2[:], AF.Exp, scale=-1.0), [Tat], [Tat])
                    P.act(lambda e, bko=bko, at1=at1: e.activation(at1[:], bko.ap[0:64, :], AF.Copy), [bko], [Tat])
                    for hf in range(2):
                        for c in range(2):
                            col = (hf * 2 + c) * 128
                            ch = 2 * g + c
                            P.pool(lambda e, hf=hf, ch=ch, col=col, tt=tt, at1=at1, at2=at2: e.tensor_tensor(
                                attnT[hf * 64:(hf + 1) * 64, ch, tt * 128:(tt + 1) * 128], at1[:, col:col + 128], at2[:, col:col + 128], ALU.mult),
                                [Tat], [TattnT[ch][tt]])
                by = nbank()

                def mmy(e, by=by, Ub=Ub, Hbr=Hbr, Hbi=Hbi):
                    r = None
                    for gp in range(16):
                        o = by.ap[:, gp * 32:(gp + 1) * 32]
                        e.matmul(o, Tbf[:, gp, :], Ub[:, gp, :], start=True, stop=False)
                        e.matmul(o, QRb[:, gp, :], Hbr[:, gp, 0:32], start=False, stop=False)
                        r = e.matmul(o, QIb[:, gp, :], Hbi[:, gp, 0:32], start=False, stop=True)
                    return r
                P.pe(mmy, [TTbf, TQb, TUb, THb], [by])
                P.act(lambda e, by=by: e.activation(ysb[:], by.ap[:, :], AF.Gelu_apprx_tanh), [by], [Tysb])

            def I_Dt(tt):
                P.dve(lambda e: e.transpose(ztk[:], ysb[:]), [Tysb], [Tztk])

            def I_Db(tt):
                bz = nbank()
                pbz = bz.ap[:].bitcast(BF16)

                def trz(e, pbz=pbz):
                    r = None
                    for fc in range(4):
                        r = e.transpose(pbz[:, fc * 128:(fc + 1) * 128], ztk[:, fc * 128:(fc + 1) * 128], identb[:])
                    return r
                P.pe(trz, [Tztk, Tidb], [bz])
                P.act(lambda e, pbz=pbz, tt=tt: e.activation(
                    zT[:, :, tt * 128:(tt + 1) * 128].rearrange("p f (k t) -> p f t k", t=4),
                    pbz[:, 0:512].rearrange("p (f t k) -> p f t k", f=4, t=4), AF.Copy),
                    [bz], [TzT[tt]])

            def run_if(fn, t_):
                if 0 <= t_ < ntiles:
                    fn(t_)
            run_if(I_A0, 0)
            run_if(I_As, 0)
            for step in range(ntiles + 5):
                run_if(I_A0, step + 1)
                run_if(I_A, step)
                run_if(I_Cs, step - 3)
                run_if(I_D, step - 4)
                run_if(I_C, step - 3)
                run_if(I_Db, step - 5)
                run_if(I_B2, step - 2)
                run_if(I_B, step - 1)
                run_if(I_Ab, step)
                run_if(I_As, step + 1)
                run_if(I_Dt, step - 4)

            if "attn" in dbg_o:
                da = s1("da", [128, 4, 256]); Tda = T(da)
                P.act(lambda e: e.activation(da[:, :, 0:128], attnT[:, :, 5 * 128:6 * 128], AF.Copy), [TattnT[c][5] for c in range(4)], [Tda])
                P.act(lambda e: e.activation(da[:, :, 128:256], attnT[:, :, 16 * 128:17 * 128], AF.Copy), [TattnT[c][16] for c in range(4)], [Tda])
                P.dma(dbg_o["attn"].rearrange("(c p) n -> p c n", p=128), da[:], [Tda], [], is_out=True)
            if "zT" in dbg_o:
                dz = s1("dz", [128, 4, 256]); Tdz = T(dz)
                P.act(lambda e: e.activation(dz[:, :, 0:128], zT[:, :, 5 * 128:6 * 128], AF.Copy), [TzT[5]], [Tdz])
                P.act(lambda e: e.activation(dz[:, :, 128:256], zT[:, :, 16 * 128:17 * 128], AF.Copy), [TzT[16]], [Tdz])
                P.dma(dbg_o["zT"].rearrange("(c p) n -> p c n", p=128), dz[:], [Tdz], [], is_out=True)
            P.barrier()

        if stop_after == 1:
            P.emit()
            return nc

        def mgap(k, tc):
            return attnT[:, k, tc] if k < 4 else zT[:, k - 4, tc]

        def Tmg_all(tt):
            return [TattnT[c][tt] for c in range(4)] + [TzT[tt]]
        wo = sb("wo", [128, 8, 1024], BF16); Two = T(wo)
        wp = sb("wp", [128, 2, 1024], BF16); Twp = T(wp)

        def prefetch_iib():
            P.dma(wo[:], w_out.rearrange("(k p) n -> p k n", p=128), [], [Two], q="pool")
            P.dma(wp[:], w_pp.rearrange("(k p) n -> p k n", p=128), [], [Twp], q="pool")
        with ExitStack() as S2:
            def s2(name, shape, dt=F32):
                return sb(name, shape, dt, S2)

            def wload2(name, src, kch, ncols):
                t = s2(name, [128, kch, ncols], BF16)
                tt_ = T(t, name)
                P.dma(t[:], src.rearrange("(k p) n -> p k n", p=128), [], [tt_], q="pool")
                return t, tt_
            wgl, Twgl = wload2("wgl", w_glu, 4, 512)
            wza, Twza = wload2("wza", w_in[:, 768:1280], 8, 512)
            wzb, Twzb = wload2("wzb", w_in[:, 1792:2304], 8, 512)
            wga, Twga = wload2("wga", w_in[:, 2304:3328], 8, 1024)
            wgb, Twgb = wload2("wgb", w_in[:, 3328:4352], 8, 1024)
            woa, Twoa = wload2("woa", w_oa, 4, 1024)
            wos, Twos = wload2("wos", w_os, 4, 1024)
            xt2 = [s2("xtb%d" % i, [128, 1024]) for i in range(2)]; Txt2 = [T(t) for t in xt2]
            xn2 = [s2("xnb%d" % i, [128, 1024], BF16) for i in range(2)]; Txn2 = [T(t) for t in xn2]
            junk = s2("junkb", [128, 1024], BF16); Tjunk = T(junk)
            ss2 = [s2("ssb%d" % i, [128, 4]) for i in range(2)]; Tss2 = [T(t) for t in ss2]
            xnTb2 = [s2("xnTb%d" % i, [128, 8, 512], BF16) for i in range(2)]; TxnTb2 = [T(t) for t in xnTb2]
            sla = s2("sla", [128, 4, 512], BF16); Tsla = T(sla)
            slb = s2("slb", [128, 4, 512], BF16); Tslb = T(slb)
            sga = s2("sga", [128, 8, 512], BF16); Tsga = T(sga)
            sgb = s2("sgb", [128, 8, 512], BF16); Tsgb = T(sgb)
            ag = s2("ag", [128, 4, 512], BF16); Tag = T(ag)
            gl = s2("gl", [128, 4, 512], BF16); Tgl = T(gl)
            sg = s2("sg", [128, 4, 512], BF16); Tsg = T(sg)
            m1b = [s2("m1_%d" % i, [128, 512]) for i in range(2)]; m2b = [s2("m2_%d" % i, [128, 512]) for i in range(2)]
            Tm1b = [T(t) for t in m1b]; Tm2b = [T(t) for t in m2b]
            blocks = [(0, 4), (4, 8), (8, 12), (12, 16), (16, 17)]

            def A_A(b):
                t0, t1 = blocks[b]
                for tt in range(t0, t1):
                    bi = tt % 2
                    load_norm_tile(tt, xt2[bi], Txt2[bi], xn2[bi], Txn2[bi], xnTb2[b % 2], TxnTb2[b % 2], (tt - t0) * 128,
                                   ss2[bi], Tss2[bi], junk, Tjunk)

            gcount = [0]

            def A_B(b):
                t0, t1 = blocks[b]
                N = (t1 - t0) * 128
                xnT, TxnT = xnTb2[b % 2], TxnTb2[b % 2]
                nxt = list(range(*blocks[b + 1])) if b + 1 < len(blocks) else []
                n0 = blocks[b + 1][0] if nxt else 0
                gcount[0] = 0

                def hook_pre():
                    gi = gcount[0]
                    if gi % 6 == 0 and gi // 6 < len(nxt):
                        tt = nxt[gi // 6]
                        bi = tt % 2
                        norm_front(tt, xt2[bi], Txt2[bi], xn2[bi], Txn2[bi], ss2[bi], Tss2[bi], junk, Tjunk)

                def hook_post():
                    gi = gcount[0]
                    if gi % 6 == 5 and gi // 6 < len(nxt):
                        tt = nxt[gi // 6]
                        bi = tt % 2
                        norm_back(tt, xn2[bi], Txn2[bi], xnTb2[(b + 1) % 2], TxnTb2[(b + 1) % 2], (tt - n0) * 128)
                    gcount[0] = gi + 1

                def proj(wt, Twt, ncol_chunks, func, dst, Tdst):
                    for c in range(ncol_chunks):
                        hook_pre()
                        bk = nbank()

                        def mm(e, bk=bk, c=c, wt=wt, xnT=xnT, N=N):
                            r = None
                            for k in range(8):
                                r = e.matmul(bk.ap[:, 0:N], wt[:, k, c * 128:(c + 1) * 128], xnT[:, k, 0:N], start=(k == 0), stop=(k == 7))
                            return r
                        P.pe(mm, [Twt, TxnT], [bk])
                        P.act(lambda e, bk=bk, c=c, dst=dst, func=func, N=N: e.activation(dst[:, c, 0:N], bk.ap[:, 0:N], func), [bk], [Tdst])
                        hook_post()
                bc = slice(t0 * 128, t1 * 128)
                rd_at = [TattnT[c][tt] for c in range(4) for tt in range(t0, t1)]
                rd_z = [TzT[tt] for tt in range(t0, t1)]
                A_C(b)
                proj(wza, Twza, 4, AF.Silu, sla, Tsla)
                P.dve(lambda e, bc=bc, N=N: e.tensor_tensor(ag[:, :, 0:N], attnT[:, :, bc], sla[:, :, 0:N], ALU.mult), rd_at + [Tsla], [Tag])
                proj(wzb, Twzb, 4, AF.Silu, slb, Tslb)
                P.dve(lambda e, bc=bc, N=N: e.tensor_tensor(sg[:, :, 0:N], zT[:, :, bc], gl[:, :, 0:N], ALU.mult), rd_z + [Tgl], [Tsg])
                P.pool(lambda e, N=N: e.tensor_tensor(sg[:, :, 0:N], sg[:, :, 0:N], slb[:, :, 0:N], ALU.mult), [Tsg, Tslb], [Tsg])
                proj(wga, Twga, 8, AF.Sigmoid, sga, Tsga)
                proj(wgb, Twgb, 8, AF.Sigmoid, sgb, Tsgb)

            def A_C(b):
                t0, t1 = blocks[b]
                N = (t1 - t0) * 128
                bc = slice(t0 * 128, t1 * 128)
                rd_z = [TzT[tt] for tt in range(t0, t1)]
                for fo in range(4):
                    bk = nbank()

                    def mmg(e, bk=bk, fo=fo, bc=bc, N=N):
                        r = None
                        for fi in range(4):
                            r = e.matmul(bk.ap[:, 0:N], wgl[:, fi, fo * 128:(fo + 1) * 128], zT[:, fi, bc], start=(fi == 0), stop=(fi == 3))
                        return r
                    P.pe(mmg, [Twgl] + rd_z, [bk])
                    P.act(lambda e, bk=bk, fo=fo, N=N: e.activation(gl[:, fo, 0:N], bk.ap[:, 0:N], AF.Sigmoid), [bk], [Tgl])

            def A_D(b):
                t0, t1 = blocks[b]
                N = (t1 - t0) * 128
                bc = slice(t0 * 128, t1 * 128)
                for mo in range(8):
                    bka = nbank()
                    bks = nbank()

                    def mmo2(e, bka=bka, bks=bks, mo=mo, N=N):
                        r = None
                        for fi in range(4):
                            e.matmul(bka.ap[:, 0:N], woa[:, fi, mo * 128:(mo + 1) * 128], ag[:, fi, 0:N], start=(fi == 0), stop=(fi == 3))
                        for fi in range(4):
                            r = e.matmul(bks.ap[:, 0:N], wos[:, fi, mo * 128:(mo + 1) * 128], sg[:, fi, 0:N], start=(fi == 0), stop=(fi == 3))
                        return r
                    P.pe(mmo2, [Twoa, Twos, Tag, Tsg], [bka, bks])
                    m1, m2, Tm1, Tm2 = m1b[mo % 2], m2b[mo % 2], Tm1b[mo % 2], Tm2b[mo % 2]
                    P.dve(lambda e, bka=bka, mo=mo, N=N, m1=m1: e.tensor_tensor(m1[:, 0:N], bka.ap[:, 0:N], sga[:, mo, 0:N], ALU.mult), [bka, Tsga], [Tm1])
                    P.dve(lambda e, bks=bks, mo=mo, N=N, m2=m2: e.tensor_tensor(m2[:, 0:N], bks.ap[:, 0:N], sgb[:, mo, 0:N], ALU.mult), [bks, Tsgb], [Tm2])
                    wr = [TattnT[mo][tt] for tt in range(t0, t1)] if mo < 4 else [TzT[tt] for tt in range(t0, t1)]
                    P.pool(lambda e, mo=mo, bc=bc, N=N, m1=m1, m2=m2: e.tensor_tensor(mgap(mo, bc), m1[:, 0:N], m2[:, 0:N], ALU.add), [Tm1, Tm2], wr)

            A_A(0)
            for b in range(len(blocks)):
                if b == 2:
                    prefetch_iib()
                A_B(b)
                A_D(b)
            P.barrier()

        with ExitStack() as S3:
            def s3(name, shape, dt=F32):
                return sb(name, shape, dt, S3)

            def wload3(name, src, kch, ncols):
                t = s3(name, [128, kch, ncols], BF16)
                tt_ = T(t, name)
                P.dma(t[:], src.rearrange("(k p) n -> p k n", p=128), [], [tt_], q="pool")
                return t, tt_
            wg, Twg = wload3("wg", w_pg, 8, 1024)
            fg_bc = s3("fg_bc", [128, 1024]); Tfg = T(fg_bc)
            P.dma(fg_bc[:], dram_bc_rows(fgain, 128), [], [Tfg])
            xt2 = [s3("xtc%d" % i, [128, 1024]) for i in range(2)]; Txt2 = [T(t) for t in xt2]
            pt2 = [s3("ptc%d" % i, [128, 256]) for i in range(2)]; Tpt2 = [T(t) for t in pt2]
            ptb2 = [s3("ptb%d" % i, [128, 256], BF16) for i in range(2)]; Tptb2 = [T(t) for t in ptb2]
            pTt2 = [s3("pTt%d" % i, [128, 2, 128], BF16) for i in range(2)]; TpTt2 = [T(t) for t in pTt2]
            h4 = [s3("h%d" % i, [128, 1024]) for i in range(4)]; Th4 = [T(t) for t in h4]
            hb2 = [s3("hb%d" % i, [128, 1024], BF16) for i in range(2)]; Thb2 = [T(t) for t in hb2]
            hT2 = [s3("hT%d" % i, [128, 8, 128], BF16) for i in range(2)]; ThT2 = [T(t) for t in hT2]
            g2t2 = [s3("g2t%d" % i, [128, 1024]) for i in range(2)]; Tg22 = [T(t) for t in g2t2]
            h22 = [s3("h2%d" % i, [128, 1024]) for i in range(2)]; Th22 = [T(t) for t in h22]
            junk = s3("junkc", [128, 1024]); Tjunk = T(junk)
            ssc2 = [s3("ssc%d" % i, [128, 4]) for i in range(2)]; Tssc2 = [T(t) for t in ssc2]
            yo2 = [s3("yo%d" % i, [128, 1024]) for i in range(2)]; Tyo2 = [T(t) for t in yo2]

            def B_A(tt):
                bi = tt % 2
                xt, Txt = xt2[bi], Txt2[bi]
                pt, Tpt = pt2[bi], Tpt2[bi]
                h, Th = h4[tt % 4], Th4[tt % 4]
                hb, Thb = hb2[bi], Thb2[bi]
                tc = slice(tt * 128, (tt + 1) * 128)
                P.dma(xt[:], xall[tc, :], [], [Txt])
                P.dma(pt[:], pall[tc, :], [], [Tpt])
                bh = [nbank(), nbank()]

                def mmh(e, bh=bh, tc=tc):
                    r = None
                    for half in range(2):
                        for k in range(8):
                            r = e.matmul(bh[half].ap[:, :], mgap(k, tc), wo[:, k, half * 512:(half + 1) * 512], start=(k == 0), stop=(k == 7))
                    return r
                P.pe(mmh, Tmg_all(tt) + [Two], bh)
                for half in range(2):
                    P.dve(lambda e, half=half, bh=bh, xt=xt, h=h: e.tensor_tensor(h[:, half * 512:(half + 1) * 512], bh[half].ap[:, :], xt[:, half * 512:(half + 1) * 512], ALU.add),
                          [bh[half], Txt], [Th])

            def B_A2(tt):
                bi = tt % 2
                pt, Tpt = pt2[bi], Tpt2[bi]
                h, Th = h4[tt % 4], Th4[tt % 4]
                hb, Thb = hb2[bi], Thb2[bi]
                P.act(lambda e, hb=hb, h=h: e.activation(hb[:], h[:], AF.Copy), [Th], [Thb])
                P.act(lambda e, pt=pt, ptb=ptb2[bi]: e.activation(ptb[:], pt[:], AF.Copy), [Tpt], [Tptb2[bi]])

            def B_B(tt):
                bi = tt % 2
                hb, Thb = hb2[bi], Thb2[bi]
                hT, ThT = hT2[bi], ThT2[bi]
                ptb, Tptb = ptb2[bi], Tptb2[bi]
                pTt, TpTt = pTt2[bi], TpTt2[bi]
                bk = nbank()
                pb = bk.ap[:].bitcast(BF16)

                def trh(e, pb=pb, hb=hb):
                    r = None
                    for k in range(8):
                        r = e.transpose(pb[:, k * 128:(k + 1) * 128], hb[:, k * 128:(k + 1) * 128], identb[:])
                    return r
                P.pe(trh, [Thb, Tidb], [bk])
                P.act(lambda e, pb=pb, hT=hT: e.activation(hT[:], pb[:, 0:1024].rearrange("p (k t) -> p k t", k=8), AF.Copy), [bk], [ThT])
                bk2 = nbank()
                pb2 = bk2.ap[:].bitcast(BF16)

                def trp(e, pb2=pb2, ptb=ptb):
                    r = None
                    for k in range(2):
                        r = e.transpose(pb2[:, k * 128:(k + 1) * 128], ptb[:, k * 128:(k + 1) * 128], identb[:])
                    return r
                P.pe(trp, [Tptb, Tidb], [bk2])
                P.act(lambda e, pb2=pb2, pTt=pTt: e.activation(pTt[:], pb2[:, 0:256].rearrange("p (k t) -> p k t", k=2), AF.Copy), [bk2], [TpTt])

            def B_C(tt):
                bi = tt % 2
                hT, ThT = hT2[bi], ThT2[bi]
                pTt, TpTt = pTt2[bi], TpTt2[bi]
                g2t, Tg2 = g2t2[bi], Tg22[bi]
                h2, Th2 = h22[bi], Th22[bi]
                h, Th = h4[tt % 4], Th4[tt % 4]
                bg = [nbank(), nbank()]

                def mmg2(e, bg=bg, hT=hT):
                    r = None
                    for half in range(2):
                        for k in range(8):
                            r = e.matmul(bg[half].ap[:, :], hT[:, k, :], wg[:, k, half * 512:(half + 1) * 512], start=(k == 0), stop=(k == 7))
                    return r
                P.pe(mmg2, [ThT, Twg], bg)
                for half in range(2):
                    P.act(lambda e, half=half, bg=bg, g2t=g2t: e.activation(g2t[:, half * 512:(half + 1) * 512], bg[half].ap[:, :], AF.Sigmoid), [bg[half]], [Tg2])
                bp = [nbank(), nbank()]

                def mmp(e, bp=bp, pTt=pTt):
                    r = None
                    for half in range(2):
                        for k in range(2):
                            r = e.matmul(bp[half].ap[:, :], pTt[:, k, :], wp[:, k, half * 512:(half + 1) * 512], start=(k == 0), stop=(k == 1))
                    return r
                P.pe(mmp, [TpTt, Twp], bp)
                for half in range(2):
                    hs = slice(half * 512, (half + 1) * 512)
                    P.dve(lambda e, half=half, bp=bp, hs=hs, h2=h2, g2t=g2t: e.tensor_tensor(h2[:, hs], bp[half].ap[:, :], g2t[:, hs], ALU.mult), [bp[half], Tg2], [Th2])
                P.pool(lambda e, h2=h2, h=h: e.tensor_tensor(h2[:], h2[:], h[:], ALU.add), [Th2, Th], [Th2])

            def B_D(tt):
                bi = tt % 2
                h2, Th2 = h22[bi], Th22[bi]
                ssc, Tssc = ssc2[bi], Tssc2[bi]
                yo, Tyo = yo2[bi], Tyo2[bi]
                tc = slice(tt * 128, (tt + 1) * 128)
                P.act(lambda e, h2=h2, ssc=ssc: e.activation(junk[:], h2[:], AF.Square, scale=1.0 / 32, accum_out=ssc[:, 0:1]), [Th2], [Tjunk, Tssc])
                P.act(lambda e, ssc=ssc: e.activation(ssc[:, 3:4], ssc[:, 0:1], AF.Ln, bias=EPS), [Tssc], [Tssc])
                P.act(lambda e, ssc=ssc: e.activation(ssc[:, 2:3], ssc[:, 3:4], AF.Exp, scale=-0.5), [Tssc], [Tssc])
                P.dve(lambda e, yo=yo, h2=h2, ssc=ssc: e.scalar_tensor_tensor(yo[:], h2[:], ssc[:, 2:3], fg_bc[:], ALU.mult, ALU.mult), [Th2, Tssc, Tfg], [Tyo])
                P.dma(y_o[tc, :], yo[:], [Tyo], [], is_out=True)

            def run3(fn, t_):
                if 0 <= t_ < NT:
                    fn(t_)
            for step in range(NT + 4):
                run3(B_C, step - 3)
                run3(B_B, step - 2)
                run3(B_A2, step - 1)
                run3(B_A, step)
                run3(B_D, step - 4)
        P.emit()
    return nc


def _consts():
    f32 = np.float32
    half = 8
    inv = np.power(np.float32(500000.0), -(np.arange(half, dtype=f32) * np.float32(2.0) / np.float32(16))).astype(f32)
    pos = np.zeros(NTOK, f32)
    pos[:2048] = np.arange(2048, dtype=f32)
    pos[2048:2048 + 64] = 1024 + np.arange(64, dtype=f32)
    ang = (pos[:, None] * inv[None, :]).astype(f32)
    cos = np.cos(ang).astype(f32)
    sin = np.sin(ang).astype(f32)
    ropetok = np.concatenate([cos, sin], axis=1).astype(f32)
    maskT = np.zeros((128, 128), f32)
    for s in range(4):
        for t in range(4):
            if t >= s:
                for g2 in range(2):
                    r0 = s * 32 + g2 * 16
                    c0 = t * 32 + g2 * 16
                    maskT[r0:r0 + 16, c0:c0 + 16] = 1.0
    return dict(ident_b=np.eye(128, dtype=f32).astype(ml_dtypes.bfloat16), ident_f=np.eye(128, dtype=f32),
                ropetok=ropetok, maskT=maskT, kidx=np.arange(64, dtype=f32),
                jtab=np.array([-1, -2, -3, -4, 1, 2, 3, 4], dtype=f32))


def _lg(a):
    return np.ascontiguousarray(np.asarray(a, np.float32).reshape(16, 2, 64).transpose(1, 2, 0).reshape(128, 16))


def _lg3(a):
    return np.ascontiguousarray(np.asarray(a, np.float32).reshape(16, 2, 64, 16).transpose(1, 2, 0, 3).reshape(128, 256))


def _ulg(a):
    return np.ascontiguousarray(np.asarray(a).reshape(2, 64, 16).transpose(2, 0, 1).reshape(32, 64))


def make_in_maps(inp):
    f32 = np.float32
    c = _consts()
    maps = []
    for b in range(8):
        xall = np.zeros((NTOK, 1024), f32)
        xall[:2048] = inp["x_prompt"][b]
        xall[2048:2112] = inp["x_sample"][b]
        pall = np.zeros((NTOK, 256), f32)
        pall[:2048] = inp["p_prompt"][0, b]
        pall[2048:2112] = inp["p_sample"][0, b]
        m = dict(c)
        m.update(
            xall=xall, pall=pall,
            ck=np.ascontiguousarray(inp["cache_attn_k"][0, b].reshape(128, 128)),
            cv=np.ascontiguousarray(inp["cache_attn_v"][0, b].reshape(128, 128)),
            h0r=_lg(inp["state_ssm_re"][0, b]),
            h0i=_lg(inp["state_ssm_im"][0, b]),
            norm_gain=np.ascontiguousarray(inp["norm_gain"][0]),
            fgain=np.ascontiguousarray(inp["final_norm_gain"]),
            w_in=np.ascontiguousarray(inp["w_in"][0]),
            sinks=np.ascontiguousarray(inp["attn_sinks"][0]),
            w_oa=np.ascontiguousarray(inp["w_o_attn"][0]),
            a_re=_lg(inp["ssm_a_re"][0]),
            a_im=_lg(inp["ssm_a_im"][0]),
            log_dt=_lg(np.repeat(inp["ssm_log_dt"][0][:, None], 64, axis=1)),
            b_re=_lg3(inp["ssm_b_re"][0]),
            b_im=_lg3(inp["ssm_b_im"][0]),
            c_re=_lg3(inp["ssm_c_re"][0].transpose(0, 2, 1)),
            c_im=_lg3(inp["ssm_c_im"][0].transpose(0, 2, 1)),
            ssm_d=np.ascontiguousarray(np.tile(inp["ssm_d"][0].reshape(16, 32).T, (4, 1))),
            w_glu=np.ascontiguousarray(inp["ssm_w_glu"][0]),
            w_os=np.ascontiguousarray(inp["w_o_ssm"][0]),
            w_out=np.ascontiguousarray(inp["w_out"][0]),
            w_pg=np.ascontiguousarray(inp["w_ple_gate"][0]),
            w_pp=np.ascontiguousarray(inp["w_ple_proj"][0]),
        )
        maps.append(m)
    return maps


_NC_CACHE = {}


def kernel(**inputs):
    inp = {k: np.asarray(v) for k, v in inputs.items()}
    if "nc" not in _NC_CACHE:
        _NC_CACHE["nc"] = build()
    nc = _NC_CACHE["nc"]
    maps = make_in_maps(inp)
    res = run_bass_kernel_spmd(nc, maps, core_ids=list(range(8)))
    R = res.results
    f32 = np.float32
    y_p = np.stack([R[b]["y"][:2048] for b in range(8)]).astype(f32)
    y_s = np.stack([R[b]["y"][2048:2112] for b in range(8)]).astype(f32)

    def st(name, shape):
        if shape == (32, 64):
            return np.stack([_ulg(R[b][name]) for b in range(8)])[None].astype(f32)
        return np.stack([R[b][name].reshape(shape) for b in range(8)])[None].astype(f32)
    return (y_p, y_s,
            st("kp", (128, 2, 64)), st("vp", (128, 2, 64)), st("srp", (32, 64)), st("sip", (32, 64)),
            st("ks", (128, 2, 64)), st("vs", (128, 2, 64)), st("srs", (32, 64)), st("sis", (32, 64)))
```
